# Optimizing a Trainium2 kernel written in Bass

```python
import jax, jax.numpy as jnp
from jax import lax
import numpy as np


D_MODEL = 1024
BATCH = 16
SEQ = 2048
DEPTH = 2

HEAD_DIM = 64
EPS = 1e-6
GLA_HEADS = 4
GLA_DV = D_MODEL // (2 * GLA_HEADS)
GLA_DK = GLA_DV // 2
GLA_RANK = 16
GLA_TAU = 16.0
GLA_CHUNK = 64
DSW_HEADS = D_MODEL // (2 * HEAD_DIM)
DSW_PATTERNS = ((128, 1), (512, 4), (2048, 16))
DSW_BLOCK = 128
CONV_CH = D_MODEL // 2
CONV_WIDTH = 31
SB_HEADS = D_MODEL // (2 * HEAD_DIM)
SB_BLOCK = 128
D_FF = ((8 * D_MODEL // 3 + 127) // 128) * 128
FFN_CONV = 3
ROPE_THETA = 500000.0
ROPE_DIMS = HEAD_DIM // 4
IN0_WIDTH = 2 * GLA_HEADS * GLA_DK + 2 * GLA_HEADS * GLA_DV + GLA_RANK + 3 * DSW_HEADS * HEAD_DIM
IN1_WIDTH = 2 * CONV_CH + 3 * SB_HEADS * HEAD_DIM
OUT0_WIDTH = GLA_HEADS * GLA_DV + DSW_HEADS * HEAD_DIM
OUT1_WIDTH = CONV_CH + SB_HEADS * HEAD_DIM

kernel_name = 'hybrid_gla_dilated_conformer_stickbreak'


def _offsets(*sizes):
    out, acc = [], 0
    for s in sizes[:-1]:
        acc += s
        out.append(acc)
    return out


def _rmsnorm(x, g):
    xf = x.astype(jnp.float32)
    y = xf * lax.rsqrt(jnp.mean(xf * xf, axis=-1, keepdims=True) + EPS)
    return (y * g).astype(x.dtype)


def _layernorm(x, g, b):
    xf = x.astype(jnp.float32)
    mu = jnp.mean(xf, axis=-1, keepdims=True)
    var = jnp.mean(jnp.square(xf - mu), axis=-1, keepdims=True)
    return ((xf - mu) * lax.rsqrt(var + EPS) * g + b).astype(x.dtype)


def _heads(t, n):
    B, S, W = t.shape
    return t.reshape(B, S, n, W // n).transpose(0, 2, 1, 3)


def _merge_heads(t):
    B, n, S, d = t.shape
    return t.transpose(0, 2, 1, 3).reshape(B, S, n * d)


def _causal_depthwise_conv(x, w):
    K, C = w.shape
    return lax.conv_general_dilated(
        x, w[:, None, :].astype(x.dtype), window_strides=(1,), padding=[(K - 1, 0)],
        dimension_numbers=('NWC', 'WIO', 'NWC'), feature_group_count=C)


def _rope_partial(x):
    S = x.shape[-2]
    half = ROPE_DIMS // 2
    inv = ROPE_THETA ** (-jnp.arange(half, dtype=jnp.float32) / half)
    ang = jnp.arange(S, dtype=jnp.float32)[:, None] * inv[None, :]
    cos, sin = jnp.cos(ang).astype(x.dtype), jnp.sin(ang).astype(x.dtype)
    x1, x2, xp = x[..., :half], x[..., half:ROPE_DIMS], x[..., ROPE_DIMS:]
    return jnp.concatenate([x1 * cos - x2 * sin, x2 * cos + x1 * sin, xp], axis=-1)


def _gla(q, k, v, log_a):
    B, H, S, DK = q.shape
    DV = v.shape[-1]
    C = GLA_CHUNK
    N = S // C
    f32 = jnp.float32
    rs = lambda t: t.astype(f32).reshape(B, H, N, C, t.shape[-1])
    qc, kc, vc, ac = rs(q * DK ** -0.5), rs(k), rs(v), rs(log_a)
    bcum = jnp.cumsum(ac, axis=3)
    btot = bcum[..., -1, :]
    q_dec = qc * jnp.exp(bcum)
    k_inv = kc * jnp.exp(-bcum)
    k_tail = kc * jnp.exp(btot[..., None, :] - bcum)
    causal = jnp.tril(jnp.ones((C, C), dtype=bool))
    scores = jnp.where(causal, jnp.einsum('bhncd,bhnsd->bhncs', q_dec, k_inv), 0.0)
    o_intra = jnp.einsum('bhncs,bhnse->bhnce', scores, vc)
    kv = jnp.einsum('bhnsd,bhnse->bhnde', k_tail, vc)

    def step(state, inp):
        dec, kv_n = inp
        return state * dec[..., None] + kv_n, state

    init = jnp.zeros((B, H, DK, DV), f32)
    _, states = lax.scan(step, init, (jnp.moveaxis(jnp.exp(btot), 2, 0), jnp.moveaxis(kv, 2, 0)))
    states = jnp.moveaxis(states, 0, 2)
    o_inter = jnp.einsum('bhncd,bhnde->bhnce', q_dec, states)
    return (o_intra + o_inter).reshape(B, H, S, DV)


def _dilated_branch(q, k, v, window, dilation):
    B, H, S, D = q.shape
    L = S // dilation
    W = window // dilation
    C = DSW_BLOCK
    nb = -(-L // C)
    pad = nb * C - L

    def to_blocks(t):
        t = t.reshape(B, H, L, dilation, D).swapaxes(2, 3)
        t = jnp.pad(t, ((0, 0), (0, 0), (0, 0), (0, pad), (0, 0)))
        return t.reshape(B, H, dilation, nb, C, D)

    def with_prev(t):
        prev = jnp.pad(t[:, :, :, :-1], ((0, 0), (0, 0), (0, 0), (1, 0), (0, 0), (0, 0)))
        return jnp.concatenate([prev, t], axis=4)

    qb = to_blocks(q)
    kc, vc = with_prev(to_blocks(k)), with_prev(to_blocks(v))
    s = jnp.einsum('bhrnqd,bhrnkd->bhrnqk', qb, kc).astype(jnp.float32) * (D ** -0.5)
    qi = jnp.arange(C)[:, None]
    ki = jnp.arange(2 * C)[None, :]
    dist = qi + C - ki
    blk = jnp.arange(nb)[:, None, None]
    valid = (dist >= 0) & (dist <= W) & (blk * C + ki - C >= 0)
    s = jnp.where(valid, s, -jnp.inf)
    m = jnp.max(s, axis=-1, keepdims=True)
    p = jnp.exp(s - m)
    den = jnp.sum(p, axis=-1, keepdims=True)
    num = jnp.einsum('bhrnqk,bhrnkd->bhrnqd', p, vc.astype(jnp.float32))

    def from_blocks(t):
        e = t.shape[-1]
        t = t.reshape(B, H, dilation, nb * C, e)[:, :, :, :L]
        return t.swapaxes(2, 3).reshape(B, H, S, e)

    return from_blocks(num), from_blocks(m), from_blocks(den)


def _dilated_window_attention(q, k, v):
    branches = [_dilated_branch(q, k, v, w, r) for (w, r) in DSW_PATTERNS]
    nums = jnp.stack([b[0] for b in branches])
    ms = jnp.stack([b[1] for b in branches])
    dens = jnp.stack([b[2] for b in branches])
    wts = jnp.exp(ms - jnp.max(ms, axis=0, keepdims=True))
    return jnp.sum(nums * wts, axis=0) / jnp.sum(dens * wts, axis=0)


def _stick_breaking_attention(q, k, v):
    B, H, S, D = q.shape
    nb = S // SB_BLOCK
    qb = q.reshape(B, H, nb, SB_BLOCK, D).transpose(2, 0, 1, 3, 4)
    pos_k = jnp.arange(S)
    vf = v.astype(jnp.float32)

    def block(args):
        n, qn = args
        z = jnp.einsum('bhqd,bhkd->bhqk', qn, k).astype(jnp.float32) * (D ** -0.5)
        pos_q = n * SB_BLOCK + jnp.arange(SB_BLOCK)
        valid = pos_k[None, :] < pos_q[:, None]
        log_beta = jax.nn.log_sigmoid(z)
        log_1m = jnp.where(valid, log_beta - z, 0.0)
        after = lax.cumsum(log_1m, axis=3, reverse=True) - log_1m
        a = jnp.where(valid, jnp.exp(log_beta + after), 0.0)
        return jnp.einsum('bhqk,bhkd->bhqd', a, vf)

    out = lax.map(block, (jnp.arange(nb), qb))
    return out.transpose(1, 2, 0, 3, 4).reshape(B, H, S, D)


def _mixer_ab(h, w_in, gla_wa2, gla_ba, gla_norm, w_out):
    aw_k, aw_v, bw = GLA_HEADS * GLA_DK, GLA_HEADS * GLA_DV, DSW_HEADS * HEAD_DIM
    aq, ak, av, ag, ar, bq, bk, bv = jnp.split(
        h @ w_in, _offsets(aw_k, aw_k, aw_v, aw_v, GLA_RANK, bw, bw, bw), axis=-1)
    log_a = jax.nn.log_sigmoid((ar @ gla_wa2 + gla_ba).astype(jnp.float32)) / GLA_TAU
    oa = _gla(_heads(aq, GLA_HEADS), _heads(ak, GLA_HEADS), _heads(av, GLA_HEADS), _heads(log_a, GLA_HEADS))
    oa = _merge_heads(_rmsnorm(oa, gla_norm)).astype(h.dtype) * jax.nn.silu(ag)
    ob = _dilated_window_attention(_rope_partial(_heads(bq, DSW_HEADS)),
                                   _rope_partial(_heads(bk, DSW_HEADS)),
                                   _heads(bv, DSW_HEADS))
    ob = _merge_heads(ob).astype(h.dtype)
    return jnp.concatenate([oa, ob], axis=-1) @ w_out


def _mixer_cd(h, w_in, conv_w, conv_b, ln_g, ln_b, w_out):
    sw = SB_HEADS * HEAD_DIM
    ca, cb, dq, dk, dv = jnp.split(h @ w_in, _offsets(CONV_CH, CONV_CH, sw, sw, sw), axis=-1)
    c = ca * jax.nn.sigmoid(cb)
    c = _causal_depthwise_conv(c, conv_w) + conv_b
    c = jax.nn.silu(_layernorm(c, ln_g, ln_b))
    od = _stick_breaking_attention(_heads(dq, SB_HEADS), _heads(dk, SB_HEADS), _heads(dv, SB_HEADS))
    od = _merge_heads(od).astype(h.dtype)
    return jnp.concatenate([c, od], axis=-1) @ w_out


def _conv_ffn(h, w_up, w_conv, w_down):
    u = _causal_depthwise_conv(h @ w_up, w_conv)
    g, val = jnp.split(u, 2, axis=-1)
    return (jax.nn.silu(g) * val) @ w_down


def setup_inputs(seed: int = 0) -> dict:
    key = jax.random.key(seed)
    ks = jax.random.split(key, 32)
    f32 = jnp.float32

    def nrm(k, shape, scale):
        return jax.random.normal(k, shape, f32) * scale

    def gain(k, n):
        return 1.0 + 0.01 * jax.random.normal(k, (n,), f32)

    out_scale = (2 * DEPTH) ** -0.5
    return {
        'x': nrm(ks[0], (BATCH, SEQ, D_MODEL), 1.0),
        'norm_mix0': gain(ks[1], D_MODEL),
        'w_in0': nrm(ks[2], (D_MODEL, IN0_WIDTH), D_MODEL ** -0.5),
        'gla_wa2': nrm(ks[3], (GLA_RANK, GLA_HEADS * GLA_DK), GLA_RANK ** -0.5),
        'gla_ba': nrm(ks[4], (GLA_HEADS * GLA_DK,), 0.01),
        'gla_norm': gain(ks[5], GLA_DV),
        'w_out0': nrm(ks[6], (OUT0_WIDTH, D_MODEL), OUT0_WIDTH ** -0.5 * out_scale),
        'norm_ffn0': gain(ks[7], D_MODEL),
        'ffn_up0': nrm(ks[8], (D_MODEL, 2 * D_FF), D_MODEL ** -0.5),
        'ffn_conv0': nrm(ks[9], (FFN_CONV, 2 * D_FF), FFN_CONV ** -0.5),
        'ffn_down0': nrm(ks[10], (D_FF, D_MODEL), D_FF ** -0.5 * out_scale),
        'norm_mix1': gain(ks[11], D_MODEL),
        'w_in1': nrm(ks[12], (D_MODEL, IN1_WIDTH), D_MODEL ** -0.5),
        'conv_w1': nrm(ks[13], (CONV_WIDTH, CONV_CH), CONV_WIDTH ** -0.5),
        'conv_b1': nrm(ks[14], (CONV_CH,), 0.01),
        'conv_ln_g1': gain(ks[15], CONV_CH),
        'conv_ln_b1': nrm(ks[16], (CONV_CH,), 0.01),
        'w_out1': nrm(ks[17], (OUT1_WIDTH, D_MODEL), OUT1_WIDTH ** -0.5 * out_scale),
        'norm_ffn1': gain(ks[18], D_MODEL),
        'ffn_up1': nrm(ks[19], (D_MODEL, 2 * D_FF), D_MODEL ** -0.5),
        'ffn_conv1': nrm(ks[20], (FFN_CONV, 2 * D_FF), FFN_CONV ** -0.5),
        'ffn_down1': nrm(ks[21], (D_FF, D_MODEL), D_FF ** -0.5 * out_scale),
        'final_norm': gain(ks[22], D_MODEL),
    }


def reference(x, norm_mix0, w_in0, gla_wa2, gla_ba, gla_norm, w_out0, norm_ffn0, ffn_up0, ffn_conv0, ffn_down0,
              norm_mix1, w_in1, conv_w1, conv_b1, conv_ln_g1, conv_ln_b1, w_out1, norm_ffn1, ffn_up1, ffn_conv1,
              ffn_down1, final_norm):
    layers = (
        (norm_mix0, (w_in0, gla_wa2, gla_ba, gla_norm, w_out0), (norm_ffn0, ffn_up0, ffn_conv0, ffn_down0)),
        (norm_mix1, (w_in1, conv_w1, conv_b1, conv_ln_g1, conv_ln_b1, w_out1), (norm_ffn1, ffn_up1, ffn_conv1, ffn_down1)),
    )
    h = x
    for i in range(DEPTH):
        g_mix, mix_p, (g_ffn, up, cw, down) = layers[i]
        mixer = _mixer_ab if i % 2 == 0 else _mixer_cd
        h = h + mixer(_rmsnorm(h, g_mix), *mix_p)
        h = h + _conv_ffn(_rmsnorm(h, g_ffn), up, cw, down)
    return _rmsnorm(h, final_norm)
```

```python
import numpy as np
import concourse.bass as bass
import concourse.mybir as mybir
from concourse.bass_utils import run_bass_kernel_spmd

F32 = mybir.dt.float32
BF16 = mybir.dt.bfloat16
AF = mybir.ActivationFunctionType
ALU = mybir.AluOpType
ESZ = {F32: 4, BF16: 2}

ENGS = ("PE", "ACT", "DVE", "POOL", "SP")


class Prog:
    def __init__(self):
        self.nc = bass.Bass("TRN2", target_bir_lowering=False)
        nc = self.nc
        self.eng = {"PE": nc.tensor, "ACT": nc.scalar, "DVE": nc.vector,
                    "POOL": nc.gpsimd, "SP": nc.sync}
        self.sem = {}
        for e in ENGS:
            self.sem[e] = nc.alloc_semaphore("s_" + e)
        self.cnt = {e: 0 for e in ENGS}
        self.dcnt = {}
        self.waited = {e: {} for e in ENGS}
        self.base = {}
        cap = 4096
        self.cap = cap
        self.rec = np.zeros((cap, 6), dtype=np.int64)
        self.rec_ev = [None] * cap
        self.alive = np.zeros(cap, dtype=bool)
        self.nrec = 0
        self.index = {}
        self.sb_off = 16384
        self.n_ps = 0
        self.n_inst = 0
        self.n_wait = 0
        self.attach_waits = True
        self._attach = None

    def sb(self, name, free, dtype, off=None):
        nbytes = free * ESZ[dtype]
        if off is None:
            off = self.sb_off
            self.sb_off = off + ((nbytes + 31) // 32) * 32
        assert off + nbytes <= 229368, (name, off, nbytes)
        t = self.nc.alloc_sbuf_tensor_at(name, [128, free], dtype, offset=off)
        self.base[t.name] = (0, off)
        return t.ap()

    def psum(self, name):
        t = self.nc.alloc_psum_tensor(name, [128, 512], F32)
        self.base[t.name] = (1, self.n_ps * 2048)
        self.n_ps += 1
        return t.ap()

    def dsem(self, name):
        if name not in self.sem:
            self.sem[name] = self.nc.alloc_semaphore("d_" + name)
            self.dcnt[name] = 0
        return name

    def _rect(self, ap):
        tn = ap.tensor.name
        if tn not in self.base:
            return None
        space, base = self.base[tn]
        pat = ap.ap
        esz = ESZ[ap.dtype]
        F = 1
        for s in list(ap.tensor.shape)[1:]:
            F *= int(s)
        off = int(ap.offset)
        p0 = off // F
        f0 = off - p0 * F
        pcnt = int(pat[0][1])
        ext = 0
        for st, c in pat[1:]:
            assert st >= 0
            ext += int(st) * (int(c) - 1)
        assert f0 + ext < F, (tn, f0, ext, F)
        b0, b1 = base + f0 * esz, base + (f0 + ext + 1) * esz
        if space == 1:
            b0 = (b0 // 2048) * 2048
            b1 = ((b1 + 2047) // 2048) * 2048
        return (space, p0, p0 + pcnt, b0, b1)

    def _overlaps(self, r):
        n = self.nrec
        if n == 0:
            return []
        R = self.rec[:n]
        m = (self.alive[:n] & (R[:, 0] == r[0]) & (R[:, 1] < r[2]) & (R[:, 2] > r[1])
             & (R[:, 3] < r[4]) & (R[:, 4] > r[3]))
        return np.nonzero(m)[0]

    def _add_rec(self, r, isw, ev):
        key = (r, isw, ev[0])
        i = self.index.get(key)
        if i is not None and self.alive[i]:
            if self.rec_ev[i][1] < ev[1]:
                self.rec_ev[i] = ev
            return
        if self.nrec >= self.cap:
            self._compact()
        i = self.nrec
        self.nrec += 1
        self.rec[i] = (r[0], r[1], r[2], r[3], r[4], isw)
        self.rec_ev[i] = ev
        self.alive[i] = True
        self.index[key] = i

    def _compact(self):
        n = self.nrec
        keep = np.nonzero(self.alive[:n])[0]
        if len(keep) > self.cap // 2:
            newcap = self.cap * 2
            rec = np.zeros((newcap, 6), dtype=np.int64)
            rec[:n] = self.rec[:n]
            self.rec = rec
            self.rec_ev = self.rec_ev + [None] * (newcap - self.cap)
            al = np.zeros(newcap, dtype=bool)
            al[:n] = self.alive[:n]
            self.alive = al
            self.cap = newcap
        evs = [self.rec_ev[i] for i in keep]
        k = len(keep)
        self.rec[:k] = self.rec[keep]
        self.alive[:] = False
        self.alive[:k] = True
        for j in range(k):
            self.rec_ev[j] = evs[j]
        self.nrec = k
        self.index = {}
        for j in range(k):
            r = tuple(int(x) for x in self.rec[j, :5])
            self.index[(r, int(self.rec[j, 5]), self.rec_ev[j][0])] = j

    def _sync(self, eng, outs, ins, is_dma=False):
        need = {}

        def add(ev, raw):
            k, v = ev
            if k == eng and not is_dma:
                if eng == "PE" or not raw:
                    return
            if need.get(k, 0) < v:
                need[k] = v

        rin = [self._rect(a) for a in ins]
        rout = [self._rect(a) for a in outs]
        for r in rin:
            if r is None:
                continue
            for i in self._overlaps(r):
                if self.rec[i, 5]:
                    add(self.rec_ev[i], True)
                elif r[0] == 1 and self.rec_ev[i][0] != eng:
                    add(self.rec_ev[i], True)
        for r in rout:
            if r is None:
                continue
            for i in self._overlaps(r):
                add(self.rec_ev[i], True)
        w = self.waited[eng]
        todo = [(k, v) for k, v in need.items() if w.get(k, 0) < v]
        attach = None
        if todo and not is_dma and self.attach_waits:
            attach = todo.pop()
        for k, v in todo:
            self.eng[eng].wait_ge(self.sem[k], v)
            self.n_wait += 1
            w[k] = v
        if attach is not None:
            w[attach[0]] = attach[1]
        self._attach = attach
        return rin, rout

    def _commit(self, rin, rout, ev):
        for r in rout:
            if r is None:
                continue
            n = self.nrec
            R = self.rec[:n]
            m = (self.alive[:n] & (R[:, 0] == r[0]) & (R[:, 1] >= r[1]) & (R[:, 2] <= r[2])
                 & (R[:, 3] >= r[3]) & (R[:, 4] <= r[4]))
            self.alive[:n][m] = False
            self._add_rec(r, 1, ev)
        for r in rin:
            if r is None:
                continue
            self._add_rec(r, 0, ev)

    def emit(self, eng, fn, outs, ins, inc=True):
        rin, rout = self._sync(eng, outs, ins)
        ins_obj = fn(self.eng[eng])
        if self._attach is not None:
            ins_obj._wait_ge(self.sem[self._attach[0]], self._attach[1])
        if inc:
            self.cnt[eng] += 1
            ins_obj.then_inc(self.sem[eng], 1)
            self._commit(rin, rout, (eng, self.cnt[eng]))
        else:
            self._commit(rin, rout, (eng, self.cnt[eng] + 1))
        self.n_inst += 1
        return ins_obj

    def dma(self, q, out, in_, sem, **kw):
        self.dsem(sem)
        rin, rout = self._sync(q, [out], [in_], is_dma=True)
        i = self.eng[q].dma_start(out=out, in_=in_, **kw)
        self.dcnt[sem] += 16
        i.then_inc(self.sem[sem], 16)
        self._commit(rin, rout, (sem, self.dcnt[sem]))
        self.n_inst += 1

    def dma_group(self, q, pairs, sem, **kw):
        self.dsem(sem)
        recs = []
        for out, in_ in pairs:
            rin, rout = self._sync(q, [out], [in_], is_dma=True)
            i = self.eng[q].dma_start(out=out, in_=in_, **kw)
            self.dcnt[sem] += 16
            i.then_inc(self.sem[sem], 16)
            recs.append((rin, rout))
            self.n_inst += 1
        for rin, rout in recs:
            self._commit(rin, rout, (sem, self.dcnt[sem]))

    def barrier(self):
        for e in ENGS:
            w = self.waited[e]
            for k in list(self.sem.keys()):
                v = self.cnt[k] if k in self.cnt else self.dcnt[k]
                if k == e or v == 0 or w.get(k, 0) >= v:
                    continue
                self.eng[e].wait_ge(self.sem[k], v)
                w[k] = v
        self.alive[:] = False
        self.nrec = 0
        self.index = {}

    def finish(self, dma_sems):
        for s in dma_sems:
            if self.waited["SP"].get(s, 0) < self.dcnt[s]:
                self.eng["SP"].wait_ge(self.sem[s], self.dcnt[s])

    def mm(self, out, lhsT, rhs, start=True, stop=True, skip=False):
        return self.emit("PE", lambda e: e.matmul(out, lhsT, rhs, start=start, stop=stop,
                                                  skip_group_check=skip),
                         [out], [lhsT, rhs])

    def act(self, out, in_, func, scale=1.0, bias=None, eng="ACT"):
        ins = [in_]
        kw = {}
        if isinstance(scale, (int, float)):
            kw["scale"] = float(scale)
        else:
            kw["scale"] = scale
            ins.append(scale)
        if bias is not None:
            kw["bias"] = bias
            if not isinstance(bias, (int, float)):
                ins.append(bias)
        return self.emit("ACT", lambda e: e.activation(out=out, in_=in_, func=func, **kw),
                         [out], ins)

    def tt(self, eng, out, in0, in1, op):
        return self.emit(eng, lambda e: e.tensor_tensor(out=out, in0=in0, in1=in1, op=op),
                         [out], [in0, in1])

    def ts(self, eng, out, in0, s1, op0, s2=None, op1=None):
        ins = [in0]
        if not isinstance(s1, (int, float)):
            ins.append(s1)
        if s2 is not None and not isinstance(s2, (int, float)):
            ins.append(s2)
        if op1 is None:
            return self.emit(eng, lambda e: e.tensor_scalar(out=out, in0=in0, scalar1=s1, scalar2=None,
                                                            op0=op0), [out], ins)
        return self.emit(eng, lambda e: e.tensor_scalar(out=out, in0=in0, scalar1=s1, scalar2=s2,
                                                        op0=op0, op1=op1), [out], ins)

    def stt(self, out, in0, scalar, in1, op0, op1):
        ins = [in0, in1]
        if not isinstance(scalar, (int, float)):
            ins.append(scalar)
        return self.emit("DVE", lambda e: e.scalar_tensor_tensor(out=out, in0=in0, scalar=scalar, in1=in1,
                                                                 op0=op0, op1=op1), [out], ins)

    def copy(self, eng, out, in_):
        if eng == "ACT":
            return self.act(out, in_, AF.Copy)
        return self.emit(eng, lambda e: e.tensor_copy(out=out, in_=in_), [out], [in_])

    def memset(self, eng, ap, val):
        return self.emit(eng, lambda e: e.memset(ap, val), [ap], [])

    def recip(self, out, in_):
        return self.emit("DVE", lambda e: e.reciprocal(out=out, in_=in_), [out], [in_])


S = 2048
D = 1024
NTT = 4
TT = 512
DFF = 2816
NFC = 22
EPS = 1e-6

P_NMIX0, P_NFFN0, P_NMIX1, P_NFFN1, P_NFIN, P_GLAN = 0, 8, 16, 24, 32, 40
P_FC0, P_FC1 = 48, 180
P_CW, P_CB, P_LG, P_LB = 312, 436, 440, 444
P_BA = 448
NPAR = 704

NSLOT = 12


def pipeline(n, stages, skews):
    mx = max(skews)
    for step in range(n + mx):
        for f, sk in zip(stages, skews):
            it = step - sk
            if 0 <= it < n:
                f(it)


class K:
    def __init__(self, nseq, stages):
        self.p = Prog()
        self.nseq = nseq
        self.stages = stages
        p = self.p
        nc = p.nc
        dt = lambda name, shape, kind="ExternalInput": nc.dram_tensor(name, shape, F32, kind=kind).ap()
        self.xT = dt("xT", [nseq, D, S])
        self.yT = dt("yT", [nseq, D, S], "ExternalOutput")
        self.params_d = dt("params", [128, NPAR])
        self.w = {}
        for name, shape in [("w_in0", [D, 3088]), ("wrot0", [D, 1024]), ("w_out0", [D, D]),
                            ("ffn_up0", [D, 2 * DFF]), ("ffn_down0", [DFF, D]),
                            ("w_in1", [D, 2560]), ("w_out1", [D, D]),
                            ("ffn_up1", [D, 2 * DFF]), ("ffn_down1", [DFF, D]),
                            ("wa2", [32, 256]), ("cosT", [128, S]), ("sinT", [128, S]),
                            ("mstrip", [128, 2176]), ("cmask", [128, 1024])]:
            self.w[name] = dt(name, shape)

        self.hn_off = p.sb_off + 8 * S * 4
        self.h = p.sb("h", 8 * S, F32).rearrange("p (c t) -> p c t", c=8)
        self.hn = p.sb("hn", 8 * S, BF16).rearrange("p (c t) -> p c t", c=8)
        self.mix_off = p.sb_off
        self.mix = p.sb("mix", 8 * S, BF16).rearrange("p (c t) -> p c t", c=8)
        self.params = p.sb("params", NPAR, F32)
        self.ones_f = p.sb("ones_f", 128, F32)
        self.ones_b = p.sb("ones_b", 128, BF16)
        self.cmask = p.sb("cmaskb", 1024, BF16)
        self.wslots = [p.sb(f"wslot{i}", 1024, BF16) for i in range(NSLOT)]
        self.wnext = 0
        self.arena_off = p.sb_off
        self.ps = [p.psum(f"ps{i}") for i in range(8)]
        print("persistent SBUF bytes", p.sb_off)

    def pcol(self, c, n=1):
        return self.params[:, c:c + n]

    def walloc(self):
        i = self.wnext
        self.wnext = (i + 1) % NSLOT
        return i

    def wload(self, dram_ap, shape3=None):
        i = self.walloc()
        a, b = dram_ap.shape[1], dram_ap.shape[2]
        dst = self.wslots[i][:, 0:a * b].rearrange("p (a b) -> p a b", a=a)
        self.p.dma("POOL", dst, dram_ap, f"w{i}")
        return dst

    def wview(self, name):
        return self.w[name].rearrange("(c p) n -> p c n", p=128)

    def prologue(self):
        p = self.p
        p.dma("SP", self.params, self.params_d, "par")
        p.memset("DVE", self.ones_f, 1.0)
        p.memset("DVE", self.ones_b, 1.0)
        p.dma("POOL", self.cmask, self.w["cmask"], "cm")
        self.ident_b = self.cmask[:, 0:128]

    def load_x(self, s):
        for tt in range(NTT):
            tsl = slice(tt * TT, (tt + 1) * TT)
            self.p.dma_group("SP", [(self.h[:, c, tsl], self.xT[s, c * 128:(c + 1) * 128, tsl]) for c in range(8)],
                             f"x{tt}")

    def store_y(self, s, src):
        for tt in range(NTT):
            tsl = slice(tt * TT, (tt + 1) * TT)
            self.p.dma_group("SP", [(self.yT[s, c * 128:(c + 1) * 128, tsl], src[:, c, tsl]) for c in range(8)],
                             f"y{tt}")

    def rmsnorm(self, gcol, out_bf16=True, out=None):
        p = self.p
        out = self.hn if out is None else out
        sq = p.sb("rn_sq", 2 * TT, BF16, off=self.arena_off).rearrange("p (b t) -> p b t", b=2)
        rstd = p.sb("rn_rstd", 2 * TT, F32, off=self.arena_off + 2 * TT * 4).rearrange("p (b t) -> p b t", b=2)
        for tt in range(NTT):
            tsl = slice(tt * TT, (tt + 1) * TT)
            ps = self.ps[tt % 2]
            for c in range(8):
                b = c % 2
                p.act(sq[:, b, :], self.h[:, c, tsl], AF.Square)
                p.mm(ps, self.ones_b, sq[:, b, :], start=(c == 0), stop=(c == 7))
            r = rstd[:, tt % 2, :]
            p.act(r, ps, AF.Ln, scale=1.0 / D, bias=EPS)
            p.act(r, r, AF.Exp, scale=-0.5)
            for c in range(8):
                p.stt(out[:, c, tsl], self.h[:, c, tsl], self.pcol(gcol + c), r, ALU.mult, ALU.mult)

    def ffn(self, L):
        p = self.p
        up = self.wview(f"ffn_up{L}")
        down = self.w[f"ffn_down{L}"]
        fc = P_FC0 if L == 0 else P_FC1
        self.rmsnorm(P_NFFN0 if L == 0 else P_NFFN1)
        a0 = self.arena_off
        NB = 3
        cg = [p.sb(f"f_cg{i}", TT, F32, off=a0 + i * 2048) for i in range(NB)]
        sg = [p.sb(f"f_sg{i}", TT, F32, off=a0 + 6144 + i * 2048) for i in range(NB)]
        cv = [p.sb(f"f_cv{i}", TT, F32, off=a0 + 12288 + i * 2048) for i in range(NB)]
        b0 = [p.sb(f"f_b0{i}", TT + 2, F32, off=a0 + 18432 + i * 2080) for i in range(NB)]
        b1 = [p.sb(f"f_b1{i}", TT + 2, F32, off=a0 + 24672 + i * 2080) for i in range(NB)]
        b2 = [p.sb(f"f_b2{i}", TT, F32, off=a0 + 30912 + i * 2048) for i in range(NB)]
        hg = [p.sb(f"f_hg{i}", 2, F32, off=a0 + 37056 + i * 32) for i in range(2)]
        act = self.mix
        groups = [list(range(0, 8)), list(range(8, 16)), list(range(16, 22))]
        gbase = 0
        for grp in groups:
            items = [(j, ci, tt) for j, ci in enumerate(grp) for tt in range(NTT)]
            wq = {}

            def need(ci):
                if ci not in wq and ci in grp:
                    wq[ci] = (self.wload(up[:, :, ci * 128:(ci + 1) * 128]),
                              self.wload(up[:, :, DFF + ci * 128:DFF + (ci + 1) * 128]))

            def P1(it, gbase=gbase):
                j, ci, tt = items[it]
                g = gbase + it
                b = g % NB
                if tt == 0:
                    need(ci)
                    if j + 1 < len(grp):
                        need(grp[j + 1])
                wg, wv = wq[ci]
                tsl = slice(tt * TT, (tt + 1) * TT)
                pg, pv = self.ps[b], self.ps[3 + b]
                w0g, w1g, w2g = (self.pcol(fc + ci * 3 + k) for k in range(3))
                w0v, w1v, w2v = (self.pcol(fc + (NFC + ci) * 3 + k) for k in range(3))
                for k in range(8):
                    p.mm(pg, wg[:, k, :], self.hn[:, k, tsl], start=(k == 0), stop=(k == 7))
                for k in range(8):
                    p.mm(pv, wv[:, k, :], self.hn[:, k, tsl], start=(k == 0), stop=(k == 7))
                p.act(b0[b][:, 2:TT + 2], pv, AF.Copy, scale=w0v)
                p.act(b1[b][:, 1:TT + 1], pv, AF.Copy, scale=w1v)
                p.act(b2[b], pv, AF.Copy, scale=w2v)
                c_ = cg[b]
                hprev = hg[(tt + 1) % 2]
                if tt == 0:
                    p.memset("DVE", c_[:, 0:2], 0.0)
                else:
                    p.ts("DVE", c_[:, 0:2], hprev[:, 0:2], w0g, ALU.mult)
                    p.stt(c_[:, 0:1], hprev[:, 1:2], w1g, c_[:, 0:1], ALU.mult, ALU.add)
                p.ts("DVE", c_[:, 2:TT], pg[:, 0:TT - 2], w0g, ALU.mult)
                p.stt(c_[:, 1:TT], pg[:, 0:TT - 1], w1g, c_[:, 1:TT], ALU.mult, ALU.add)
                p.stt(c_[:, 0:TT], pg[:, 0:TT], w2g, c_[:, 0:TT], ALU.mult, ALU.add)
                if tt < NTT - 1:
                    p.copy("DVE", hg[tt % 2][:, 0:2], pg[:, TT - 2:TT])

            def P2(it, gbase=gbase):
                j, ci, tt = items[it]
                g = gbase + it
                b = g % NB
                bp = (g - 1) % NB
                p.act(sg[b], cg[b], AF.Silu)
                if tt == 0:
                    p.memset("POOL", b0[b][:, 0:2], 0.0)
                    p.memset("POOL", b1[b][:, 0:1], 0.0)
                else:
                    p.copy("POOL", b0[b][:, 0:2], b0[bp][:, TT:TT + 2])
                    p.copy("POOL", b1[b][:, 0:1], b1[bp][:, TT:TT + 1])
                p.tt("POOL", cv[b], b0[b][:, 0:TT], b1[b][:, 0:TT], ALU.add)
                p.tt("POOL", cv[b], cv[b], b2[b], ALU.add)

            def P3(it, gbase=gbase):
                j, ci, tt = items[it]
                b = (gbase + it) % NB
                tsl = slice(tt * TT, (tt + 1) * TT)
                p.tt("DVE", act[:, j, tsl], sg[b], cv[b], ALU.mult)
                if tt == NTT - 1:
                    wq.pop(ci)

            pipeline(len(items), [P1, P2, P3], [0, 1, 2])
            gbase += len(items)
            G = len(grp)
            dview = down[grp[0] * 128:(grp[0] + G) * 128, :].rearrange("(j p) n -> p j n", p=128)
            wd_next = self.wload(dview[:, :, 0:128])
            for n in range(8):
                wd = wd_next
                if n + 1 < 8:
                    wd_next = self.wload(dview[:, :, (n + 1) * 128:(n + 2) * 128])
                for tt in range(NTT):
                    tsl = slice(tt * TT, (tt + 1) * TT)
                    ps = self.ps[6 + (n * NTT + tt) % 2]
                    for j in range(G):
                        p.mm(ps, wd[:, j, :], act[:, j, tsl], start=(j == 0), stop=(j == G - 1))
                    p.tt("DVE", self.h[:, n, tsl], self.h[:, n, tsl], ps, ALU.add)

    def final_norm_store(self, s):
        p = self.p
        of = p.sb("outf", 8 * S, F32, off=self.hn_off).rearrange("p (c t) -> p c t", c=8)
        self.rmsnorm(P_NFIN, out=of)
        self.store_y(s, of)


def build(nseq, stages):
    k = K(nseq, stages)
    p = k.p
    k.prologue()
    for s in range(nseq):
        k.load_x(s)
        if "mix0" in stages:
            mixer0(k)
        if "ffn0" in stages:
            k.ffn(0)
        if "mix1" in stages:
            mixer1(k)
        if "ffn1" in stages:
            k.ffn(1)
        if "final" in stages:
            k.final_norm_store(s)
        else:
            k.store_y(s, k.h)
    p.finish([f"y{t}" for t in range(NTT)])
    print("instructions", p.n_inst, "waits", p.n_wait, "counts", p.cnt)
    return k


A_Q, A_K, A_V, A_G, A_R, B_Q, B_K, B_V = 0, 256, 512, 1024, 1536, 1552, 2064, 2576
C_A, C_B, D_Q, D_K, D_V = 0, 512, 1024, 1536, 2048


def cm(self, i, rows=slice(0, 128), cols=slice(0, 128)):
    return self.cmask[rows, i * 128:(i + 1) * 128][:, cols]


def proj_fm(self, w, M, tt, ps):
    p = self.p
    tsl = slice(tt * TT, (tt + 1) * TT)
    for k in range(8):
        p.mm(ps[0:M, :], w[:, k, 0:M], self.hn[:, k, tsl], start=(k == 0), stop=(k == 7))


def proj_tm(self, w, N, tb, ps):
    p = self.p
    for k in range(8):
        p.mm(ps[:, 0:N], self.hn[:, k, tb * 128:(tb + 1) * 128], w[:, k, 0:N], start=(k == 0), stop=(k == 7))


def out_proj(self, wname):
    p = self.p
    wv = self.wview(wname)
    nxt = self.wload(wv[:, :, 0:128])
    for n in range(8):
        w = nxt
        if n < 7:
            nxt = self.wload(wv[:, :, (n + 1) * 128:(n + 2) * 128])
        for tt in range(NTT):
            tsl = slice(tt * TT, (tt + 1) * TT)
            ps = self.ps[6 + (n * NTT + tt) % 2]
            for k in range(8):
                p.mm(ps, w[:, k, :], self.mix[:, k, tsl], start=(k == 0), stop=(k == 7))
            p.tt("DVE", self.h[:, n, tsl], self.h[:, n, tsl], ps, ALU.add)


def dsw(self):
    p = self.p
    a0 = self.arena_off
    cosT = p.sb("d_cos", S, F32, off=a0)
    sinT = p.sb("d_sin", S, F32, off=a0 + 8192)
    mstrip = p.sb("d_mstrip", 2176, BF16, off=a0 + 16384)
    o = a0 + 16384 + 4352
    qT = p.sb("d_qT", S, BF16, off=o)
    kTz = [p.sb(f"d_kTz{e}", S, BF16, off=o + 4096 + e * 4096) for e in range(2)]
    o += 4096
    v = p.sb("d_v", S, BF16, off=o + 8192).rearrange("p (b d) -> p b d", b=16)
    t1s = [p.sb(f"d_t1{i}", TT, F32, off=o + 12288 + i * 2048) for i in range(2)]
    t2s = [p.sb(f"d_t2{i}", TT, F32, off=o + 16384 + i * 2048) for i in range(2)]
    p.memset("POOL", kTz[0][64:128, :], 0.0)
    p.memset("POOL", kTz[1][0:64, :], 0.0)
    p.dma("SP", cosT, self.w["cosT"], "cos")
    p.dma("SP", sinT, self.w["sinT"], "sin")
    p.dma("POOL", mstrip, self.w["mstrip"], "ms")
    w0 = self.wview("w_in0")
    wr = self.wview("wrot0")
    NB = 3
    pt = [p.sb(f"d_pt{i}", TT, BF16, off=o + 20480 + i * 1024) for i in range(NB)]
    pm = [p.sb(f"d_pm{i}", TT, BF16, off=o + 23552 + i * 1024) for i in range(NB)]
    rden = t1s[0]
    def dsw_w(pr):
        return (self.wload(w0[:, :, B_Q + pr * 128:B_Q + (pr + 1) * 128]),
                self.wload(wr[:, :, pr * 128:(pr + 1) * 128]),
                self.wload(w0[:, :, B_K + pr * 128:B_K + (pr + 1) * 128]),
                self.wload(wr[:, :, 512 + pr * 128:512 + (pr + 1) * 128]),
                self.wload(w0[:, :, B_V + pr * 128:B_V + (pr + 1) * 128]))
    wnext = dsw_w(0)
    for pr in range(4):
        wq, wqr, wk, wkr, wv = wnext
        pj = [(wa, wb, dst, tt) for (wa, wb, dst) in ((wq, wqr, qT), (wk, wkr, None)) for tt in range(NTT)]

        def J0(it):
            wa, wb, dst, tt = pj[it]
            proj_fm(self, wa, 128, tt, self.ps[0] if it % 2 == 0 else self.ps[5])
            proj_fm(self, wb, 128, tt, self.ps[6 + it % 2])

        def J1(it):
            wa, wb, dst, tt = pj[it]
            tsl = slice(tt * TT, (tt + 1) * TT)
            p.tt("DVE", t1s[it % 2], self.ps[0] if it % 2 == 0 else self.ps[5], cosT[:, tsl], ALU.mult)
            p.tt("DVE", t2s[it % 2], self.ps[6 + it % 2], sinT[:, tsl], ALU.mult)

        def J2(it):
            wa, wb, dst, tt = pj[it]
            tsl = slice(tt * TT, (tt + 1) * TT)
            t1, t2 = t1s[it % 2], t2s[it % 2]
            if dst is not None:
                p.tt("POOL", dst[:, tsl], t1, t2, ALU.add)
            else:
                p.tt("POOL", kTz[0][0:64, tsl], t1[0:64, :], t2[0:64, :], ALU.add)
                p.tt("POOL", kTz[1][64:128, tsl], t1[64:128, :], t2[64:128, :], ALU.add)

        pipeline(len(pj), [J2, J1, J0], [2, 1, 0])
        for tb in range(16):
            ps = self.ps[0] if tb % 2 == 0 else self.ps[5]
            proj_tm(self, wv, 128, tb, ps)
            p.act(v[:, tb, :], ps[:, 0:128], AF.Copy)
        if pr + 1 < 4:
            wnext = dsw_w(pr + 1)
        items = []
        for qt in range(4):
            for kb in range(4 * qt + 4):
                for e in range(2):
                    items.append((qt, kb, e))

        def geom(it):
            qt, kb, e = items[it]
            q0 = max(qt * TT, kb * 128)
            N = (qt + 1) * TT - q0
            return qt, kb, e, slice(64 * e, 64 * e + 64), q0, N, q0 - qt * TT, q0 - kb * 128

        def D0(it):
            qt, kb, e, rows, q0, N, c0, x0 = geom(it)
            pss = self.ps[5 + it % NB]
            p.mm(pss[:, 0:N], kTz[e][:, kb * 128:(kb + 1) * 128], qT[:, q0:q0 + N])

        def D1(it):
            qt, kb, e, rows, q0, N, c0, x0 = geom(it)
            b = it % NB
            p.act(pt[b][:, 0:N], self.ps[5 + b][:, 0:N], AF.Exp, scale=0.125)

        def D2(it):
            qt, kb, e, rows, q0, N, c0, x0 = geom(it)
            b = it % NB
            p.tt("DVE", pm[b][:, 0:N], pt[b][:, 0:N], mstrip[:, x0:x0 + N], ALU.mult)

        def D3(it):
            qt, kb, e, rows, q0, N, c0, x0 = geom(it)
            b = it % NB
            num, den = self.ps[1 + e], self.ps[3 + e]
            nkb = 4 * qt + 4
            p.mm(num[:, c0:c0 + N], v[:, kb, :], pm[b][:, 0:N], start=(kb == 0), stop=(kb == nkb - 1))
            p.mm(den[:, c0:c0 + N], self.ones_b, pm[b][:, 0:N], start=(kb == 0), stop=(kb == nkb - 1))
            if kb == nkb - 1:
                tsl = slice(qt * TT, (qt + 1) * TT)
                p.act(rden[rows, :], den[rows, :], AF.Ln)
                p.act(rden[rows, :], rden[rows, :], AF.Exp, scale=-1.0)
                p.tt("DVE", self.mix[rows, 4 + pr, tsl], num[rows, :], rden[rows, :], ALU.mult)

        pipeline(len(items), [D3, D2, D1, D0], [3, 2, 1, 0])


def gla(self):
    p = self.p
    a0 = self.arena_off
    l_hi = p.sb("g_lhi", 16 * 256, BF16, off=a0).rearrange("p (b d) -> p b d", b=16)
    l_lo = p.sb("g_llo", 16 * 256, BF16, off=a0 + 8192).rearrange("p (b d) -> p b d", b=16)
    o = a0 + 16384
    arT = p.sb("g_arT", S, BF16, off=o)
    pre = p.sb("g_pre", 256, F32, off=o + 4096)
    lf = p.sb("g_lf", 256, F32, off=o + 5120)
    q_dec = p.sb("g_qdec", S, BF16, off=o)
    k_inv = p.sb("g_kinv", S, BF16, off=o + 4096)
    k_tail = p.sb("g_ktail", 16 * 64, BF16, off=o + 8192).rearrange("p (b d) -> p b d", b=16)
    v = p.sb("g_v", 16 * 128, BF16, off=o + 10240).rearrange("p (b d) -> p b d", b=16)
    e_pos = p.sb("g_epos", TT, F32, off=o + 14336)
    e_neg = p.sb("g_eneg", TT, F32, off=o + 16384)
    sq = p.sb("g_sq", TT, F32, off=o + 14336)
    r = p.sb("g_r", TT, F32, off=o + 16384)
    sgt = p.sb("g_sg", TT, F32, off=o + 18432)
    t1 = p.sb("g_t1", TT, F32, off=o + 20480)
    wgt = [p.sb(f"g_wgt{i}", 64, F32, off=o + 22528 + i * 256) for i in range(2)]
    scm = [p.sb(f"g_scm{i}", 128, BF16, off=o + 23040 + i * 256) for i in range(2)]
    Sf2 = [p.sb(f"g_sf{i}", 128, F32, off=o + 23552 + i * 512) for i in range(2)]
    dec = p.sb("g_dec", 32, F32, off=o + 24576)
    Sall = p.sb("g_sall", 32 * 128, BF16, off=o + 24704).rearrange("p (n d) -> p n d", n=32)
    o_sb = p.sb("g_osb", S, F32, off=self.mix_off + 4 * S * 2)
    w0 = self.wview("w_in0")
    war = self.wload(w0[:, :, A_R:A_R + 16])
    i = self.walloc()
    wa2 = self.wslots[i][0:16, 0:256]
    p.dma("POOL", wa2, self.w["wa2"][0:16, :], f"w{i}")
    for tt in range(NTT):
        ps = self.ps[tt % 2]
        proj_fm(self, war, 16, tt, ps)
        p.act(arT[0:16, tt * TT:(tt + 1) * TT], ps[0:16, :], AF.Copy)
    for tb in range(16):
        ps = self.ps[2 + tb % 2]
        p.mm(ps[:, 0:256], arT[0:16, tb * 128:(tb + 1) * 128], wa2)
        p.tt("DVE", pre, ps[:, 0:256], self.params[:, P_BA:P_BA + 256], ALU.add)
        p.act(pre, pre, AF.Exp, scale=-1.0)
        p.act(lf, pre, AF.Ln, bias=1.0)
        p.act(l_hi[:, tb, :], pre, AF.Ln, bias=1.0)
        p.tt("DVE", l_lo[:, tb, :], lf, l_hi[:, tb, :], ALU.subtract)
    def gla_w(h):
        return (self.wload(w0[:, :, A_Q + h * 64:A_Q + (h + 1) * 64]),
                self.wload(w0[:, :, A_K + h * 64:A_K + (h + 1) * 64]),
                self.wload(w0[:, :, A_V + h * 128:A_V + (h + 1) * 128]),
                self.wload(w0[:, :, A_G + h * 128:A_G + (h + 1) * 128]))
    gnext = gla_w(0)
    for h in range(4):
        hs = slice(h * 64, (h + 1) * 64)
        wq, wk, wv, wg = gnext
        for tt in range(NTT):
            tsl = slice(tt * TT, (tt + 1) * TT)
            pcs = self.ps[3 * (tt % 2)]
            for j in range(4):
                tb = tt * 4 + j
                p.mm(pcs[0:64, j * 128:(j + 1) * 128], l_hi[:, tb, hs], cm(self, 1), start=True, stop=False)
                p.mm(pcs[0:64, j * 128:(j + 1) * 128], l_lo[:, tb, hs], cm(self, 1), start=False, stop=True)
            p.act(e_pos[0:64, :], pcs[0:64, :], AF.Exp, scale=-1.0 / 16)
            p.act(e_neg[0:64, :], pcs[0:64, :], AF.Exp, scale=1.0 / 16)
            pq, pk = self.ps[3 * (tt % 2) + 1], self.ps[3 * (tt % 2) + 2]
            proj_fm(self, wq, 64, tt, pq)
            proj_fm(self, wk, 64, tt, pk)
            p.stt(q_dec[0:64, tsl], pq[0:64, :], 0.125, e_pos[0:64, :], ALU.mult, ALU.mult)
            p.tt("DVE", k_inv[0:64, tsl], pk[0:64, :], e_neg[0:64, :], ALU.mult)
            p.copy("DVE", dec[0:64, tt * 8:(tt + 1) * 8],
                   e_pos[0:64, :].rearrange("p (n c) -> p n c", c=64)[:, :, 63])
        for tb in range(16):
            b = tb % 2
            pkt, pD, pv = (self.ps[3], self.ps[4], self.ps[5]) if tb % 2 == 0 else (self.ps[0], self.ps[1], self.ps[2])
            proj_tm(self, wk, 64, tb, pkt)
            p.mm(pD[:, 0:64], cm(self, 2), l_hi[:, tb, hs], start=True, stop=False)
            p.mm(pD[:, 0:64], cm(self, 2), l_lo[:, tb, hs], start=False, stop=True)
            p.act(wgt[b], pD[:, 0:64], AF.Exp, scale=-1.0 / 16)
            p.tt("DVE", k_tail[:, tb, :], pkt[:, 0:64], wgt[b], ALU.mult)
            proj_tm(self, wv, 128, tb, pv)
            p.act(v[:, tb, :], pv[:, 0:128], AF.Copy)
        if h + 1 < 4:
            gnext = gla_w(h + 1)
        p.memset("DVE", Sf2[0][0:64, :], 0.0)
        for n in range(31):
            tb, j = n // 2, n % 2
            pkv = self.ps[n % 4]
            p.mm(pkv[0:64, 0:128], k_tail[64 * j:64 * j + 64, tb, :], v[64 * j:64 * j + 64, tb, :])
            p.stt(Sf2[(n + 1) % 2][0:64, :], Sf2[n % 2][0:64, :], dec[0:64, n:n + 1], pkv[0:64, 0:128],
                  ALU.mult, ALU.add)
            p.act(Sall[0:64, n + 1, :], Sf2[(n + 1) % 2][0:64, :], AF.Copy)
        for tb in range(16):
            b = tb % 2
            bsl = slice(tb * 128, (tb + 1) * 128)
            psc, pso = self.ps[4 + b], self.ps[6 + b]
            p.mm(psc[:, 0:128], k_inv[0:64, bsl], q_dec[0:64, bsl])
            p.tt("DVE", scm[b], psc[:, 0:128], cm(self, 1), ALU.mult)
            p.mm(pso[:, 0:128], v[:, tb, :], scm[b], start=True, stop=False)
            for j in range(2):
                n = 2 * tb + j
                csl = slice(tb * 128 + 64 * j, tb * 128 + 64 * j + 64)
                if n > 0:
                    p.mm(pso[:, 64 * j:64 * j + 64], Sall[0:64, n, :], q_dec[0:64, csl],
                         start=False, stop=(j == 1))
            p.act(o_sb[:, bsl], pso[:, 0:128], AF.Copy)
        r_all = p.sb(f"g_rall{h}", 4 * TT, F32, off=o).rearrange("p (a t) -> p a t", a=4)
        for tt in range(NTT):
            tsl = slice(tt * TT, (tt + 1) * TT)
            pss = self.ps[2 + tt % 2]
            p.act(sq, o_sb[:, tsl], AF.Square)
            p.mm(pss, self.ones_f, sq)
            p.act(r_all[:, tt, :], pss, AF.Ln, scale=1.0 / 128, bias=EPS)
            p.act(r_all[:, tt, :], r_all[:, tt, :], AF.Exp, scale=-0.5)
        for tt in range(NTT):
            tsl = slice(tt * TT, (tt + 1) * TT)
            pg = self.ps[4 + tt % 2]
            proj_fm(self, wg, 128, tt, pg)
            p.act(sgt, pg, AF.Silu)
            p.stt(t1, o_sb[:, tsl], self.pcol(P_GLAN), r_all[:, tt, :], ALU.mult, ALU.mult)
            p.tt("DVE", self.mix[:, h, tsl], t1, sgt, ALU.mult)


def mixer0(self):
    self.rmsnorm(P_NMIX0)
    gla(self)
    dsw(self)
    out_proj(self, "w_out0")


def conformer(self):
    p = self.p
    a0 = self.arena_off
    cbuf = p.sb("c_cbuf", 4 * 542, BF16, off=a0).rearrange("p (c t) -> p c t", c=4)
    o = a0 + 4352
    ysb = p.sb("c_ysb", 4 * TT, F32, off=o).rearrange("p (c t) -> p c t", c=4)
    sig = p.sb("c_sig", TT, F32, off=o + 8192)
    sq = p.sb("c_sq", TT, F32, off=o + 10240)
    mean = p.sb("c_mean", TT, F32, off=o + 12288)
    msq = p.sb("c_msq", TT, F32, off=o + 14336)
    r = p.sb("c_r", TT, F32, off=o + 16384)
    dg = [p.sb(f"c_dg{i}", 31 * 128, BF16, off=o + 18432 + i * 7936) for i in range(2)]
    w1 = self.wview("w_in1")
    wca = [self.wload(w1[:, :, C_A + c * 128:C_A + (c + 1) * 128]) for c in range(4)]
    wcb = [self.wload(w1[:, :, C_B + c * 128:C_B + (c + 1) * 128]) for c in range(4)]
    def build_dg(dgb, c):
        for k in range(31):
            p.ts("DVE", dgb[:, k * 128:(k + 1) * 128], cm(self, 0), self.pcol(P_CW + c * 31 + k), ALU.mult)

    sigs = [sig, p.sb("c_sig2", TT, F32, off=o + 18432 + 2 * 7936)]
    build_dg(dg[0], 0)

    def C0(it):
        tt, c = it // 4, it % 4
        pa, pb = (self.ps[0], self.ps[1]) if it % 2 == 0 else (self.ps[6], self.ps[7])
        sg_ = sigs[it % 2]
        proj_fm(self, wca[c], 128, tt, pa)
        proj_fm(self, wcb[c], 128, tt, pb)
        p.act(sg_, pb, AF.Sigmoid)
        if tt == 0:
            p.memset("POOL", cbuf[:, c, 0:30], 0.0)
        else:
            p.copy("POOL", cbuf[:, c, 0:30], cbuf[:, c, 512:542])
        p.tt("DVE", cbuf[:, c, 30:542], pa, sg_, ALU.mult)

    def C1(it):
        tt, c = it // 4, it % 4
        tsl = slice(tt * TT, (tt + 1) * TT)
        psum_s, psum_q = self.ps[4], self.ps[5]
        dgb = dg[it % 2]
        if it + 1 < 16:
            build_dg(dg[(it + 1) % 2], (it + 1) % 4)
        py = self.ps[2 + c % 2]
        for k in range(31):
            p.mm(py, dgb[:, k * 128:(k + 1) * 128], cbuf[:, c, k:k + 512], start=(k == 0), stop=(k == 30))
        p.ts("DVE", ysb[:, c, :], py, self.pcol(P_CB + c), ALU.add)
        p.mm(psum_s, self.ones_f, ysb[:, c, :], start=(c == 0), stop=(c == 3))
        p.act(sq, ysb[:, c, :], AF.Square)
        p.mm(psum_q, self.ones_f, sq, start=(c == 0), stop=(c == 3))
        if c == 3:
            p.ts("DVE", mean, psum_s, 1.0 / 512, ALU.mult)
            p.tt("DVE", msq, mean, mean, ALU.mult)
            p.stt(r, psum_q, 1.0 / 512, msq, ALU.mult, ALU.subtract)
            p.act(r, r, AF.Ln, bias=EPS)
            p.act(r, r, AF.Exp, scale=-0.5)
            for cc in range(4):
                p.tt("DVE", ysb[:, cc, :], ysb[:, cc, :], mean, ALU.subtract)
                p.tt("DVE", ysb[:, cc, :], ysb[:, cc, :], r, ALU.mult)
                p.act(self.mix[:, cc, tsl], ysb[:, cc, :], AF.Silu, scale=self.pcol(P_LG + cc),
                      bias=self.pcol(P_LB + cc))

    pipeline(16, [C0, C1], [0, 1])


def pipeline(n, stages, skews):
    mx = max(skews)
    for step in range(n + mx):
        for f, sk in zip(stages, skews):
            it = step - sk
            if 0 <= it < n:
                f(it)


def stickbreak(self):
    p = self.p
    a0 = self.arena_off
    qT = p.sb("s_qT", S, BF16, off=a0)
    kTz = [p.sb(f"s_kTz{e}", S, BF16, off=a0 + 4096 + e * 4096) for e in range(2)]
    v = p.sb("s_v", S, BF16, off=a0 + 12288).rearrange("p (b d) -> p b d", b=16)
    o = a0 + 16384
    p.memset("POOL", kTz[0][64:128, :], 0.0)
    p.memset("POOL", kTz[1][0:64, :], 0.0)
    NB = 3
    E = [p.sb(f"s_E{i}", TT, F32, off=o + i * 2048) for i in range(NB)]
    SPb = [p.sb(f"s_SP{i}", TT, BF16, off=o + 6144 + i * 1024) for i in range(NB)]
    R = [p.sb(f"s_R{i}", TT, F32, off=o + 9216 + i * 2048) for i in range(2)]
    arg = [p.sb(f"s_arg{i}", TT, F32, off=o + 13312 + i * 2048) for i in range(NB)]
    at = [p.sb(f"s_a{i}", TT, BF16, off=o + 19456 + i * 1024) for i in range(NB)]
    zeros_b = p.sb("s_zeros", 128, BF16, off=o + 22528)
    p.memset("DVE", zeros_b, 0.0)
    w1 = self.wview("w_in1")
    def sb_w(pr):
        return (self.wload(w1[:, :, D_Q + pr * 128:D_Q + (pr + 1) * 128]),
                self.wload(w1[:, :, D_K + pr * 128:D_K + (pr + 1) * 128]),
                self.wload(w1[:, :, D_V + pr * 128:D_V + (pr + 1) * 128]))
    snext = sb_w(0)
    for pr in range(4):
        wq, wk, wv = snext
        for (wa, dst) in ((wq, qT), (wk, None)):
            for tt in range(NTT):
                ps = self.ps[tt % 2]
                tsl = slice(tt * TT, (tt + 1) * TT)
                proj_fm(self, wa, 128, tt, ps)
                if dst is not None:
                    p.act(dst[:, tsl], ps, AF.Copy)
                else:
                    p.act(kTz[0][0:64, tsl], ps[0:64, :], AF.Copy)
                    p.act(kTz[1][64:128, tsl], ps[64:128, :], AF.Copy)
        for tb in range(16):
            ps = self.ps[tb % 2]
            proj_tm(self, wv, 128, tb, ps)
            p.act(v[:, tb, :], ps[:, 0:128], AF.Copy)
        if pr + 1 < 4:
            snext = sb_w(pr + 1)
        items = []
        for qt in range(4):
            for kb in range(4 * qt + 3, -1, -1):
                for e in range(2):
                    items.append((qt, kb, e))

        def geom(it):
            qt, kb, e = items[it]
            q0 = max(qt * TT, kb * 128)
            N = (qt + 1) * TT - q0
            return qt, kb, e, slice(64 * e, 64 * e + 64), q0, N, q0 - qt * TT, kb >= 4 * qt

        def S0(it):
            qt, kb, e, rows, q0, N, c0, diag = geom(it)
            pz = self.ps[it % 4]
            p.mm(pz[:, 0:N], kTz[e][:, kb * 128:(kb + 1) * 128], qT[:, q0:q0 + N], start=True, stop=not diag)
            if diag:
                p.mm(pz[:, 0:128], cm(self, 0), cm(self, 7), start=False, stop=True)

        def S1(it):
            qt, kb, e, rows, q0, N, c0, diag = geom(it)
            b = it % NB
            pz = self.ps[it % 4]
            p.act(E[b][:, 0:N], pz[:, 0:N], AF.Exp, scale=0.125)
            p.act(SPb[b][:, 0:N], E[b][:, 0:N], AF.Ln, bias=1.0)

        def S2(it):
            qt, kb, e, rows, q0, N, c0, diag = geom(it)
            b = it % NB
            pz, prr = self.ps[it % 4], self.ps[4 + it % 2]
            p.mm(pz[:, 0:N], cm(self, 4), SPb[b][:, 0:N], start=False, stop=True, skip=True)
            p.mm(prr[:, 0:N], cm(self, 6), SPb[b][:, 0:N])

        def S3(it):
            qt, kb, e, rows, q0, N, c0, diag = geom(it)
            b = it % NB
            pz, prr = self.ps[it % 4], self.ps[4 + it % 2]
            if kb == 4 * qt + 3:
                p.memset("POOL", R[e], 0.0)
            p.stt(arg[b][:, 0:N], pz[:, 0:N], 0.125, R[e][:, c0:c0 + N], ALU.mult, ALU.subtract)
            p.tt("DVE", R[e][:, c0:c0 + N], R[e][:, c0:c0 + N], prr[:, 0:N], ALU.add)

        def S4(it):
            qt, kb, e, rows, q0, N, c0, diag = geom(it)
            p.act(at[it % NB][:, 0:N], arg[it % NB][:, 0:N], AF.Exp)

        def S5(it):
            qt, kb, e, rows, q0, N, c0, diag = geom(it)
            po = self.ps[6 + e]
            if kb == 4 * qt + 3:
                p.mm(po, zeros_b, self.cmask[:, 0:512], start=True, stop=False)
            p.mm(po[:, c0:c0 + N], v[:, kb, :], at[it % NB][:, 0:N], start=False, stop=(kb == 0))
            if kb == 0:
                p.act(self.mix[rows, 4 + pr, qt * TT:(qt + 1) * TT], po[rows, :], AF.Copy)

        pipeline(len(items), [S5, S4, S3, S2, S1, S0], [5, 4, 3, 2, 1, 0])


def mixer1(self):
    self.rmsnorm(P_NMIX1)
    if "noconf" not in self.stages:
        conformer(self)
    if "nosb" not in self.stages:
        stickbreak(self)
    out_proj(self, "w_out1")


import numpy as np

def pcols(v):
    v = np.asarray(v, np.float32)
    return np.ascontiguousarray(v.reshape(-1, 128).T)

def host_params(I):
    P = np.zeros((128, NPAR), np.float32)
    P[:, P_NMIX0:P_NMIX0 + 8] = pcols(I["norm_mix0"])
    P[:, P_NFFN0:P_NFFN0 + 8] = pcols(I["norm_ffn0"])
    P[:, P_NMIX1:P_NMIX1 + 8] = pcols(I["norm_mix1"])
    P[:, P_NFFN1:P_NFFN1 + 8] = pcols(I["norm_ffn1"])
    P[:, P_NFIN:P_NFIN + 8] = pcols(I["final_norm"])
    P[:, P_GLAN] = np.asarray(I["gla_norm"], np.float32)
    for L, off in ((0, P_FC0), (1, P_FC1)):
        fc = np.asarray(I[f"ffn_conv{L}"], np.float32)
        a = fc.reshape(3, 44, 128).transpose(2, 1, 0)
        P[:, off:off + 132] = a.reshape(128, 132)
    cw = np.asarray(I["conv_w1"], np.float32)
    P[:, P_CW:P_CW + 124] = cw.reshape(31, 4, 128).transpose(2, 1, 0).reshape(128, 124)
    P[:, P_CB:P_CB + 4] = pcols(I["conv_b1"])
    P[:, P_LG:P_LG + 4] = pcols(I["conv_ln_g1"])
    P[:, P_LB:P_LB + 4] = pcols(I["conv_ln_b1"])
    P[:, P_BA:P_BA + 256] = np.asarray(I["gla_ba"], np.float32)[None, :]
    return P

def host_consts():
    C = {}
    half = 8
    inv = (np.float32(500000.0) ** (-np.arange(half, dtype=np.float32) / np.float32(half))).astype(np.float32)
    ang = (np.arange(S, dtype=np.float32)[:, None] * inv[None, :]).astype(np.float32)
    cos, sin = np.cos(ang).astype(np.float32), np.sin(ang).astype(np.float32)
    cosT = np.ones((128, S), np.float32)
    sinT = np.zeros((128, S), np.float32)
    for base in (0, 64):
        cosT[base:base + 8] = cos.T
        cosT[base + 8:base + 16] = cos.T
        sinT[base:base + 8] = -sin.T
        sinT[base + 8:base + 16] = sin.T
    C["cosT"], C["sinT"] = cosT, sinT
    ki = np.arange(128)[:, None]
    x = np.arange(2176)[None, :]
    d = x - ki
    c = ((d >= 0) & (d <= 128)).astype(np.float32) + ((d >= 0) & (d % 4 == 0) & (d <= 512)) + ((d >= 0) & (d % 16 == 0) & (d <= 2048))
    C["mstrip"] = c.astype(np.float32)
    cm = np.zeros((128, 8, 128), np.float32)
    i = np.arange(128)[:, None]
    j = np.arange(128)[None, :]
    cm[:, 0] = (i == j)
    cm[:, 1] = ((i // 64) == (j // 64)) & (i <= j)
    cm[:, 2] = ((i // 64) == (j // 64)) & (i > j)
    cm[:, 3] = ((i // 64) == (j // 64)) & (i <= j)
    cm[:, 4] = np.where(i >= j, -8.0, 0.0)
    cm[:, 5] = (i >= j)
    cm[:, 6] = 1.0
    cm[:, 7] = np.where(i >= j, -240000.0, 0.0)
    C["cmask"] = cm.reshape(128, 1024)
    return C

def host_inmap(I, seqs):
    x = np.asarray(I["x"], np.float32)
    m = {"xT": np.ascontiguousarray(x[seqs].transpose(0, 2, 1)), "params": host_params(I)}
    for n in ["w_in0", "w_out0", "ffn_up0", "ffn_down0", "w_in1", "w_out1", "ffn_up1", "ffn_down1"]:
        m[n] = np.ascontiguousarray(np.asarray(I[n], np.float32))
    w = m["w_in0"]
    GO = 1552
    def rot(base):
        cols = []
        for h in range(8):
            b = base + h * 64
            idx = np.arange(b, b + 64)
            idx[0:8] = np.arange(b + 8, b + 16)
            idx[8:16] = np.arange(b, b + 8)
            cols.append(idx)
        return np.concatenate(cols)
    m["wrot0"] = np.ascontiguousarray(np.concatenate([w[:, rot(GO)], w[:, rot(GO + 512)]], axis=1))
    wa2 = np.zeros((32, 256), np.float32)
    wa2[:16] = np.asarray(I["gla_wa2"], np.float32)
    m["wa2"] = wa2
    m.update(host_consts())
    return m


def kernel(**inputs):
    I = {k: np.asarray(v) for k, v in inputs.items()}
    B = I["x"].shape[0]
    ncores = 8
    per = B // ncores
    k = build(per, ["mix0", "ffn0", "mix1", "ffn1", "final"])
    base = host_inmap(I, list(range(per)))
    x = np.asarray(I["x"], np.float32)
    in_maps = []
    for c in range(ncores):
        m = dict(base)
        m["xT"] = np.ascontiguousarray(x[c * per:(c + 1) * per].transpose(0, 2, 1))
        in_maps.append(m)
    res = run_bass_kernel_spmd(k.p.nc, in_maps, core_ids=list(range(ncores)))
    out = np.empty((B, S, D), np.float32)
    for c in range(ncores):
        yT = np.asarray(res.results[c]["yT"])
        out[c * per:(c + 1) * per] = yT.transpose(0, 2, 1)
    return out
```

```python
import numpy as np
import concourse.bass as bass
import concourse.mybir as mybir
from concourse.bass_utils import run_bass_kernel_spmd

F32 = mybir.dt.float32
BF16 = mybir.dt.bfloat16
AF = mybir.ActivationFunctionType
ALU = mybir.AluOpType
ESZ = {F32: 4, BF16: 2}

ENGS = ("PE", "ACT", "DVE", "POOL", "SP")


class Prog:
    def __init__(self):
        self.nc = bass.Bass("TRN2", target_bir_lowering=False)
        nc = self.nc
        self.eng = {"PE": nc.tensor, "ACT": nc.scalar, "DVE": nc.vector,
                    "POOL": nc.gpsimd, "SP": nc.sync}
        self.sem = {}
        for e in ENGS:
            self.sem[e] = nc.alloc_semaphore("s_" + e)
        self.cnt = {e: 0 for e in ENGS}
        self.dcnt = {}
        self.waited = {e: {} for e in ENGS}
        self.base = {}
        cap = 4096
        self.cap = cap
        self.rec = np.zeros((cap, 6), dtype=np.int64)
        self.rec_ev = [None] * cap
        self.alive = np.zeros(cap, dtype=bool)
        self.nrec = 0
        self.index = {}
        self.sb_off = 16384
        self.n_ps = 0
        self.n_inst = 0
        self.n_wait = 0
        self.attach_waits = True
        self._attach = None

    def sb(self, name, free, dtype, off=None):
        nbytes = free * ESZ[dtype]
        if off is None:
            off = self.sb_off
            self.sb_off = off + ((nbytes + 31) // 32) * 32
        assert off + nbytes <= 229368, (name, off, nbytes)
        t = self.nc.alloc_sbuf_tensor_at(name, [128, free], dtype, offset=off)
        self.base[t.name] = (0, off)
        return t.ap()

    def psum(self, name):
        t = self.nc.alloc_psum_tensor(name, [128, 512], F32)
        self.base[t.name] = (1, self.n_ps * 2048)
        self.n_ps += 1
        return t.ap()

    def dsem(self, name):
        if name not in self.sem:
            self.sem[name] = self.nc.alloc_semaphore("d_" + name)
            self.dcnt[name] = 0
        return name

    def _rect(self, ap):
        tn = ap.tensor.name
        if tn not in self.base:
            return None
        space, base = self.base[tn]
        pat = ap.ap
        esz = ESZ[ap.dtype]
        F = 1
        for s in list(ap.tensor.shape)[1:]:
            F *= int(s)
        off = int(ap.offset)
        p0 = off // F
        f0 = off - p0 * F
        pcnt = int(pat[0][1])
        ext = 0
        for st, c in pat[1:]:
            assert st >= 0
            ext += int(st) * (int(c) - 1)
        assert f0 + ext < F, (tn, f0, ext, F)
        b0, b1 = base + f0 * esz, base + (f0 + ext + 1) * esz
        if space == 1:
            b0 = (b0 // 2048) * 2048
            b1 = ((b1 + 2047) // 2048) * 2048
        return (space, p0, p0 + pcnt, b0, b1)

    def _overlaps(self, r):
        n = self.nrec
        if n == 0:
            return []
        R = self.rec[:n]
        m = (self.alive[:n] & (R[:, 0] == r[0]) & (R[:, 1] < r[2]) & (R[:, 2] > r[1])
             & (R[:, 3] < r[4]) & (R[:, 4] > r[3]))
        return np.nonzero(m)[0]

    def _add_rec(self, r, isw, ev):
        key = (r, isw, ev[0])
        i = self.index.get(key)
        if i is not None and self.alive[i]:
            if self.rec_ev[i][1] < ev[1]:
                self.rec_ev[i] = ev
            return
        if self.nrec >= self.cap:
            self._compact()
        i = self.nrec
        self.nrec += 1
        self.rec[i] = (r[0], r[1], r[2], r[3], r[4], isw)
        self.rec_ev[i] = ev
        self.alive[i] = True
        self.index[key] = i

    def _compact(self):
        n = self.nrec
        keep = np.nonzero(self.alive[:n])[0]
        if len(keep) > self.cap // 2:
            newcap = self.cap * 2
            rec = np.zeros((newcap, 6), dtype=np.int64)
            rec[:n] = self.rec[:n]
            self.rec = rec
            self.rec_ev = self.rec_ev + [None] * (newcap - self.cap)
            al = np.zeros(newcap, dtype=bool)
            al[:n] = self.alive[:n]
            self.alive = al
            self.cap = newcap
        evs = [self.rec_ev[i] for i in keep]
        k = len(keep)
        self.rec[:k] = self.rec[keep]
        self.alive[:] = False
        self.alive[:k] = True
        for j in range(k):
            self.rec_ev[j] = evs[j]
        self.nrec = k
        self.index = {}
        for j in range(k):
            r = tuple(int(x) for x in self.rec[j, :5])
            self.index[(r, int(self.rec[j, 5]), self.rec_ev[j][0])] = j

    def _sync(self, eng, outs, ins, is_dma=False):
        need = {}

        def add(ev, raw):
            k, v = ev
            if k == eng and not is_dma:
                if eng == "PE" or not raw:
                    return
            if need.get(k, 0) < v:
                need[k] = v

        rin = [self._rect(a) for a in ins]
        rout = [self._rect(a) for a in outs]
        for r in rin:
            if r is None:
                continue
            for i in self._overlaps(r):
                if self.rec[i, 5]:
                    add(self.rec_ev[i], True)
                elif r[0] == 1 and self.rec_ev[i][0] != eng:
                    add(self.rec_ev[i], True)
        for r in rout:
            if r is None:
                continue
            for i in self._overlaps(r):
                add(self.rec_ev[i], True)
        w = self.waited[eng]
        todo = [(k, v) for k, v in need.items() if w.get(k, 0) < v]
        attach = None
        if todo and not is_dma and self.attach_waits:
            attach = todo.pop()
        for k, v in todo:
            self.eng[eng].wait_ge(self.sem[k], v)
            self.n_wait += 1
            w[k] = v
        if attach is not None:
            w[attach[0]] = attach[1]
        self._attach = attach
        return rin, rout

    def _commit(self, rin, rout, ev):
        for r in rout:
            if r is None:
                continue
            n = self.nrec
            R = self.rec[:n]
            m = (self.alive[:n] & (R[:, 0] == r[0]) & (R[:, 1] >= r[1]) & (R[:, 2] <= r[2])
                 & (R[:, 3] >= r[3]) & (R[:, 4] <= r[4]))
            self.alive[:n][m] = False
            self._add_rec(r, 1, ev)
        for r in rin:
            if r is None:
                continue
            self._add_rec(r, 0, ev)

    def emit(self, eng, fn, outs, ins, inc=True):
        rin, rout = self._sync(eng, outs, ins)
        ins_obj = fn(self.eng[eng])
        if self._attach is not None:
            ins_obj._wait_ge(self.sem[self._attach[0]], self._attach[1])
        if inc:
            self.cnt[eng] += 1
            ins_obj.then_inc(self.sem[eng], 1)
            self._commit(rin, rout, (eng, self.cnt[eng]))
        else:
            self._commit(rin, rout, (eng, self.cnt[eng] + 1))
        self.n_inst += 1
        return ins_obj

    def dma(self, q, out, in_, sem, **kw):
        self.dsem(sem)
        rin, rout = self._sync(q, [out], [in_], is_dma=True)
        i = self.eng[q].dma_start(out=out, in_=in_, **kw)
        self.dcnt[sem] += 16
        i.then_inc(self.sem[sem], 16)
        self._commit(rin, rout, (sem, self.dcnt[sem]))
        self.n_inst += 1

    def dma_group(self, q, pairs, sem, **kw):
        self.dsem(sem)
        recs = []
        for out, in_ in pairs:
            rin, rout = self._sync(q, [out], [in_], is_dma=True)
            i = self.eng[q].dma_start(out=out, in_=in_, **kw)
            self.dcnt[sem] += 16
            i.then_inc(self.sem[sem], 16)
            recs.append((rin, rout))
            self.n_inst += 1
        for rin, rout in recs:
            self._commit(rin, rout, (sem, self.dcnt[sem]))

    def barrier(self):
        for e in ENGS:
            w = self.waited[e]
            for k in list(self.sem.keys()):
                v = self.cnt[k] if k in self.cnt else self.dcnt[k]
                if k == e or v == 0 or w.get(k, 0) >= v:
                    continue
                self.eng[e].wait_ge(self.sem[k], v)
                w[k] = v
        self.alive[:] = False
        self.nrec = 0
        self.index = {}

    def finish(self, dma_sems):
        for s in dma_sems:
            if self.waited["SP"].get(s, 0) < self.dcnt[s]:
                self.eng["SP"].wait_ge(self.sem[s], self.dcnt[s])

    def mm(self, out, lhsT, rhs, start=True, stop=True, skip=False):
        return self.emit("PE", lambda e: e.matmul(out, lhsT, rhs, start=start, stop=stop,
                                                  skip_group_check=skip),
                         [out], [lhsT, rhs])

    def act(self, out, in_, func, scale=1.0, bias=None, eng="ACT"):
        ins = [in_]
        kw = {}
        if isinstance(scale, (int, float)):
            kw["scale"] = float(scale)
        else:
            kw["scale"] = scale
            ins.append(scale)
        if bias is not None:
            kw["bias"] = bias
            if not isinstance(bias, (int, float)):
                ins.append(bias)
        return self.emit("ACT", lambda e: e.activation(out=out, in_=in_, func=func, **kw),
                         [out], ins)

    def tt(self, eng, out, in0, in1, op):
        return self.emit(eng, lambda e: e.tensor_tensor(out=out, in0=in0, in1=in1, op=op),
                         [out], [in0, in1])

    def ts(self, eng, out, in0, s1, op0, s2=None, op1=None):
        ins = [in0]
        if not isinstance(s1, (int, float)):
            ins.append(s1)
        if s2 is not None and not isinstance(s2, (int, float)):
            ins.append(s2)
        if op1 is None:
            return self.emit(eng, lambda e: e.tensor_scalar(out=out, in0=in0, scalar1=s1, scalar2=None,
                                                            op0=op0), [out], ins)
        return self.emit(eng, lambda e: e.tensor_scalar(out=out, in0=in0, scalar1=s1, scalar2=s2,
                                                        op0=op0, op1=op1), [out], ins)

    def stt(self, out, in0, scalar, in1, op0, op1):
        ins = [in0, in1]
        if not isinstance(scalar, (int, float)):
            ins.append(scalar)
        return self.emit("DVE", lambda e: e.scalar_tensor_tensor(out=out, in0=in0, scalar=scalar, in1=in1,
                                                                 op0=op0, op1=op1), [out], ins)

    def copy(self, eng, out, in_):
        if eng == "ACT":
            return self.act(out, in_, AF.Copy)
        return self.emit(eng, lambda e: e.tensor_copy(out=out, in_=in_), [out], [in_])

    def memset(self, eng, ap, val):
        return self.emit(eng, lambda e: e.memset(ap, val), [ap], [])

    def recip(self, out, in_):
        return self.emit("DVE", lambda e: e.reciprocal(out=out, in_=in_), [out], [in_])


S = 2048
D = 1024
NTT = 4
TT = 512
DFF = 2816
NFC = 22
EPS = 1e-6

P_NMIX0, P_NFFN0, P_NMIX1, P_NFFN1, P_NFIN, P_GLAN = 0, 8, 16, 24, 32, 40
P_FC0, P_FC1 = 48, 180
P_CW, P_CB, P_LG, P_LB = 312, 436, 440, 444
P_BA = 448
NPAR = 704

NSLOT = 12


def pipeline(n, stages, skews):
    mx = max(skews)
    for step in range(n + mx):
        for f, sk in zip(stages, skews):
            it = step - sk
            if 0 <= it < n:
                f(it)


class K:
    def __init__(self, nseq, stages):
        self.p = Prog()
        self.nseq = nseq
        self.stages = stages
        p = self.p
        nc = p.nc
        dt = lambda name, shape, kind="ExternalInput": nc.dram_tensor(name, shape, F32, kind=kind).ap()
        self.xT = dt("xT", [nseq, D, S])
        self.yT = dt("yT", [nseq, D, S], "ExternalOutput")
        self.params_d = dt("params", [128, NPAR])
        self.w = {}
        for name, shape in [("w_in0", [D, 3088]), ("wrot0", [D, 1024]), ("w_out0", [D, D]),
                            ("ffn_up0", [D, 2 * DFF]), ("ffn_down0", [DFF, D]),
                            ("w_in1", [D, 2560]), ("w_out1", [D, D]),
                            ("ffn_up1", [D, 2 * DFF]), ("ffn_down1", [DFF, D]),
                            ("wa2", [32, 256]), ("cosT", [128, S]), ("sinT", [128, S]),
                            ("mstrip", [128, 2176]), ("cmask", [128, 1024])]:
            self.w[name] = dt(name, shape)

        self.hn_off = p.sb_off + 8 * S * 4
        self.h = p.sb("h", 8 * S, F32).rearrange("p (c t) -> p c t", c=8)
        self.hn = p.sb("hn", 8 * S, BF16).rearrange("p (c t) -> p c t", c=8)
        self.mix_off = p.sb_off
        self.mix = p.sb("mix", 8 * S, BF16).rearrange("p (c t) -> p c t", c=8)
        self.params = p.sb("params", NPAR, F32)
        self.ones_f = p.sb("ones_f", 128, F32)
        self.ones_b = p.sb("ones_b", 128, BF16)
        self.cmask = p.sb("cmaskb", 1024, BF16)
        self.wslots = [p.sb(f"wslot{i}", 1024, BF16) for i in range(NSLOT)]
        self.wnext = 0
        self.arena_off = p.sb_off
        self.ps = [p.psum(f"ps{i}") for i in range(8)]
        print("persistent SBUF bytes", p.sb_off)

    def pcol(self, c, n=1):
        return self.params[:, c:c + n]

    def walloc(self):
        i = self.wnext
        self.wnext = (i + 1) % NSLOT
        return i

    def wload(self, dram_ap, shape3=None):
        i = self.walloc()
        a, b = dram_ap.shape[1], dram_ap.shape[2]
        dst = self.wslots[i][:, 0:a * b].rearrange("p (a b) -> p a b", a=a)
        self.p.dma("POOL", dst, dram_ap, f"w{i}")
        return dst

    def wview(self, name):
        return self.w[name].rearrange("(c p) n -> p c n", p=128)

    def prologue(self):
        p = self.p
        p.dma("SP", self.params, self.params_d, "par")
        p.memset("DVE", self.ones_f, 1.0)
        p.memset("DVE", self.ones_b, 1.0)
        p.dma("POOL", self.cmask, self.w["cmask"], "cm")
        self.ident_b = self.cmask[:, 0:128]

    def load_x(self, s):
        for tt in range(NTT):
            tsl = slice(tt * TT, (tt + 1) * TT)
            self.p.dma_group("SP", [(self.h[:, c, tsl], self.xT[s, c * 128:(c + 1) * 128, tsl]) for c in range(8)],
                             f"x{tt}")

    def store_y(self, s, src):
        for tt in range(NTT):
            tsl = slice(tt * TT, (tt + 1) * TT)
            self.p.dma_group("SP", [(self.yT[s, c * 128:(c + 1) * 128, tsl], src[:, c, tsl]) for c in range(8)],
                             f"y{tt}")

    def rmsnorm(self, gcol, out_bf16=True, out=None):
        p = self.p
        out = self.hn if out is None else out
        sq = p.sb("rn_sq", 2 * TT, BF16, off=self.arena_off).rearrange("p (b t) -> p b t", b=2)
        rstd = p.sb("rn_rstd", 2 * TT, F32, off=self.arena_off + 2 * TT * 4).rearrange("p (b t) -> p b t", b=2)
        for tt in range(NTT):
            tsl = slice(tt * TT, (tt + 1) * TT)
            ps = self.ps[tt % 2]
            for c in range(8):
                b = c % 2
                p.act(sq[:, b, :], self.h[:, c, tsl], AF.Square)
                p.mm(ps, self.ones_b, sq[:, b, :], start=(c == 0), stop=(c == 7))
            r = rstd[:, tt % 2, :]
            p.act(r, ps, AF.Ln, scale=1.0 / D, bias=EPS)
            p.act(r, r, AF.Exp, scale=-0.5)
            for c in range(8):
                p.stt(out[:, c, tsl], self.h[:, c, tsl], self.pcol(gcol + c), r, ALU.mult, ALU.mult)

    def ffn(self, L):
        p = self.p
        up = self.wview(f"ffn_up{L}")
        down = self.w[f"ffn_down{L}"]
        fc = P_FC0 if L == 0 else P_FC1
        self.rmsnorm(P_NFFN0 if L == 0 else P_NFFN1)
        a0 = self.arena_off
        NB = 3
        cg = [p.sb(f"f_cg{i}", TT, F32, off=a0 + i * 2048) for i in range(NB)]
        sg = [p.sb(f"f_sg{i}", TT, F32, off=a0 + 6144 + i * 2048) for i in range(NB)]
        cv = [p.sb(f"f_cv{i}", TT, F32, off=a0 + 12288 + i * 2048) for i in range(NB)]
        b0 = [p.sb(f"f_b0{i}", TT + 2, F32, off=a0 + 18432 + i * 2080) for i in range(NB)]
        b1 = [p.sb(f"f_b1{i}", TT + 2, F32, off=a0 + 24672 + i * 2080) for i in range(NB)]
        b2 = [p.sb(f"f_b2{i}", TT, F32, off=a0 + 30912 + i * 2048) for i in range(NB)]
        hg = [p.sb(f"f_hg{i}", 2, F32, off=a0 + 37056 + i * 32) for i in range(2)]
        act = self.mix
        groups = [list(range(0, 8)), list(range(8, 16)), list(range(16, 22))]
        gbase = 0
        for grp in groups:
            items = [(j, ci, tt) for j, ci in enumerate(grp) for tt in range(NTT)]
            wq = {}

            def need(ci):
                if ci not in wq and ci in grp:
                    wq[ci] = (self.wload(up[:, :, ci * 128:(ci + 1) * 128]),
                              self.wload(up[:, :, DFF + ci * 128:DFF + (ci + 1) * 128]))

            def P1(it, gbase=gbase):
                j, ci, tt = items[it]
                g = gbase + it
                b = g % NB
                if tt == 0:
                    need(ci)
                    if j + 1 < len(grp):
                        need(grp[j + 1])
                wg, wv = wq[ci]
                tsl = slice(tt * TT, (tt + 1) * TT)
                pg, pv = self.ps[b], self.ps[3 + b]
                w0g, w1g, w2g = (self.pcol(fc + ci * 3 + k) for k in range(3))
                w0v, w1v, w2v = (self.pcol(fc + (NFC + ci) * 3 + k) for k in range(3))
                for k in range(8):
                    p.mm(pg, wg[:, k, :], self.hn[:, k, tsl], start=(k == 0), stop=(k == 7))
                for k in range(8):
                    p.mm(pv, wv[:, k, :], self.hn[:, k, tsl], start=(k == 0), stop=(k == 7))
                p.act(b0[b][:, 2:TT + 2], pv, AF.Copy, scale=w0v)
                p.act(b1[b][:, 1:TT + 1], pv, AF.Copy, scale=w1v)
                p.act(b2[b], pv, AF.Copy, scale=w2v)
                c_ = cg[b]
                hprev = hg[(tt + 1) % 2]
                if tt == 0:
                    p.memset("DVE", c_[:, 0:2], 0.0)
                else:
                    p.ts("DVE", c_[:, 0:2], hprev[:, 0:2], w0g, ALU.mult)
                    p.stt(c_[:, 0:1], hprev[:, 1:2], w1g, c_[:, 0:1], ALU.mult, ALU.add)
                p.ts("DVE", c_[:, 2:TT], pg[:, 0:TT - 2], w0g, ALU.mult)
                p.stt(c_[:, 1:TT], pg[:, 0:TT - 1], w1g, c_[:, 1:TT], ALU.mult, ALU.add)
                p.stt(c_[:, 0:TT], pg[:, 0:TT], w2g, c_[:, 0:TT], ALU.mult, ALU.add)
                if tt < NTT - 1:
                    p.copy("DVE", hg[tt % 2][:, 0:2], pg[:, TT - 2:TT])

            def P2(it, gbase=gbase):
                j, ci, tt = items[it]
                g = gbase + it
                b = g % NB
                bp = (g - 1) % NB
                p.act(sg[b], cg[b], AF.Silu)
                if tt == 0:
                    p.memset("POOL", b0[b][:, 0:2], 0.0)
                    p.memset("POOL", b1[b][:, 0:1], 0.0)
                else:
                    p.copy("POOL", b0[b][:, 0:2], b0[bp][:, TT:TT + 2])
                    p.copy("POOL", b1[b][:, 0:1], b1[bp][:, TT:TT + 1])
                p.tt("POOL", cv[b], b0[b][:, 0:TT], b1[b][:, 0:TT], ALU.add)
                p.tt("POOL", cv[b], cv[b], b2[b], ALU.add)

            def P3(it, gbase=gbase):
                j, ci, tt = items[it]
                b = (gbase + it) % NB
                tsl = slice(tt * TT, (tt + 1) * TT)
                p.tt("DVE", act[:, j, tsl], sg[b], cv[b], ALU.mult)
                if tt == NTT - 1:
                    wq.pop(ci)

            pipeline(len(items), [P1, P2, P3], [0, 1, 2])
            gbase += len(items)
            G = len(grp)
            dview = down[grp[0] * 128:(grp[0] + G) * 128, :].rearrange("(j p) n -> p j n", p=128)
            wd_next = self.wload(dview[:, :, 0:128])
            for n in range(8):
                wd = wd_next
                if n + 1 < 8:
                    wd_next = self.wload(dview[:, :, (n + 1) * 128:(n + 2) * 128])
                for tt in range(NTT):
                    tsl = slice(tt * TT, (tt + 1) * TT)
                    ps = self.ps[6 + (n * NTT + tt) % 2]
                    for j in range(G):
                        p.mm(ps, wd[:, j, :], act[:, j, tsl], start=(j == 0), stop=(j == G - 1))
                    p.tt("DVE", self.h[:, n, tsl], self.h[:, n, tsl], ps, ALU.add)

    def final_norm_store(self, s):
        p = self.p
        of = p.sb("outf", 8 * S, F32, off=self.hn_off).rearrange("p (c t) -> p c t", c=8)
        self.rmsnorm(P_NFIN, out=of)
        self.store_y(s, of)


def build(nseq, stages):
    k = K(nseq, stages)
    p = k.p
    k.prologue()
    for s in range(nseq):
        k.load_x(s)
        if "mix0" in stages:
            mixer0(k)
        if "ffn0" in stages:
            k.ffn(0)
        if "mix1" in stages:
            mixer1(k)
        if "ffn1" in stages:
            k.ffn(1)
        if "final" in stages:
            k.final_norm_store(s)
        else:
            k.store_y(s, k.h)
    p.finish([f"y{t}" for t in range(NTT)])
    print("instructions", p.n_inst, "waits", p.n_wait, "counts", p.cnt)
    return k


A_Q, A_K, A_V, A_G, A_R, B_Q, B_K, B_V = 0, 256, 512, 1024, 1536, 1552, 2064, 2576
C_A, C_B, D_Q, D_K, D_V = 0, 512, 1024, 1536, 2048


def cm(self, i, rows=slice(0, 128), cols=slice(0, 128)):
    return self.cmask[rows, i * 128:(i + 1) * 128][:, cols]


def proj_fm(self, w, M, tt, ps):
    p = self.p
    tsl = slice(tt * TT, (tt + 1) * TT)
    for k in range(8):
        p.mm(ps[0:M, :], w[:, k, 0:M], self.hn[:, k, tsl], start=(k == 0), stop=(k == 7))


def proj_tm(self, w, N, tb, ps):
    p = self.p
    for k in range(8):
        p.mm(ps[:, 0:N], self.hn[:, k, tb * 128:(tb + 1) * 128], w[:, k, 0:N], start=(k == 0), stop=(k == 7))


def out_proj(self, wname):
    p = self.p
    wv = self.wview(wname)
    nxt = self.wload(wv[:, :, 0:128])
    for n in range(8):
        w = nxt
        if n < 7:
            nxt = self.wload(wv[:, :, (n + 1) * 128:(n + 2) * 128])
        for tt in range(NTT):
            tsl = slice(tt * TT, (tt + 1) * TT)
            ps = self.ps[6 + (n * NTT + tt) % 2]
            for k in range(8):
                p.mm(ps, w[:, k, :], self.mix[:, k, tsl], start=(k == 0), stop=(k == 7))
            p.tt("DVE", self.h[:, n, tsl], self.h[:, n, tsl], ps, ALU.add)


def dsw(self):
    p = self.p
    a0 = self.arena_off
    cosT = p.sb("d_cos", S, F32, off=a0)
    sinT = p.sb("d_sin", S, F32, off=a0 + 8192)
    mstrip = p.sb("d_mstrip", 2176, BF16, off=a0 + 16384)
    o = a0 + 16384 + 4352
    qT = p.sb("d_qT", S, BF16, off=o)
    kTz = [p.sb(f"d_kTz{e}", S, BF16, off=o + 4096 + e * 4096) for e in range(2)]
    o += 4096
    v = p.sb("d_v", S, BF16, off=o + 8192).rearrange("p (b d) -> p b d", b=16)
    t1s = [p.sb(f"d_t1{i}", TT, F32, off=o + 12288 + i * 2048) for i in range(2)]
    t2s = [p.sb(f"d_t2{i}", TT, F32, off=o + 16384 + i * 2048) for i in range(2)]
    p.memset("POOL", kTz[0][64:128, :], 0.0)
    p.memset("POOL", kTz[1][0:64, :], 0.0)
    p.dma("SP", cosT, self.w["cosT"], "cos")
    p.dma("SP", sinT, self.w["sinT"], "sin")
    p.dma("POOL", mstrip, self.w["mstrip"], "ms")
    w0 = self.wview("w_in0")
    wr = self.wview("wrot0")
    NB = 3
    pt = [p.sb(f"d_pt{i}", TT, BF16, off=o + 20480 + i * 1024) for i in range(NB)]
    pm = [p.sb(f"d_pm{i}", TT, BF16, off=o + 23552 + i * 1024) for i in range(NB)]
    rden = t1s[0]
    def dsw_w(pr):
        return (self.wload(w0[:, :, B_Q + pr * 128:B_Q + (pr + 1) * 128]),
                self.wload(wr[:, :, pr * 128:(pr + 1) * 128]),
                self.wload(w0[:, :, B_K + pr * 128:B_K + (pr + 1) * 128]),
                self.wload(wr[:, :, 512 + pr * 128:512 + (pr + 1) * 128]),
                self.wload(w0[:, :, B_V + pr * 128:B_V + (pr + 1) * 128]))
    wnext = dsw_w(0)
    for pr in range(4):
        wq, wqr, wk, wkr, wv = wnext
        pj = [(wa, wb, dst, tt) for (wa, wb, dst) in ((wq, wqr, qT), (wk, wkr, None)) for tt in range(NTT)]

        def J0(it):
            wa, wb, dst, tt = pj[it]
            proj_fm(self, wa, 128, tt, self.ps[0] if it % 2 == 0 else self.ps[5])
            proj_fm(self, wb, 128, tt, self.ps[6 + it % 2])

        def J1(it):
            wa, wb, dst, tt = pj[it]
            tsl = slice(tt * TT, (tt + 1) * TT)
            p.tt("DVE", t1s[it % 2], self.ps[0] if it % 2 == 0 else self.ps[5], cosT[:, tsl], ALU.mult)
            p.tt("DVE", t2s[it % 2], self.ps[6 + it % 2], sinT[:, tsl], ALU.mult)

        def J2(it):
            wa, wb, dst, tt = pj[it]
            tsl = slice(tt * TT, (tt + 1) * TT)
            t1, t2 = t1s[it % 2], t2s[it % 2]
            if dst is not None:
                p.tt("POOL", dst[:, tsl], t1, t2, ALU.add)
            else:
                p.tt("POOL", kTz[0][0:64, tsl], t1[0:64, :], t2[0:64, :], ALU.add)
                p.tt("POOL", kTz[1][64:128, tsl], t1[64:128, :], t2[64:128, :], ALU.add)

        pipeline(len(pj), [J2, J1, J0], [2, 1, 0])
        for tb in range(16):
            ps = self.ps[0] if tb % 2 == 0 else self.ps[5]
            proj_tm(self, wv, 128, tb, ps)
            p.act(v[:, tb, :], ps[:, 0:128], AF.Copy)
        if pr + 1 < 4:
            wnext = dsw_w(pr + 1)
        items = []
        for qt in range(4):
            for e in range(2):
                for kb in range(4 * qt + 4):
                    items.append((qt, kb, e))

        def geom(it):
            qt, kb, e = items[it]
            q0 = max(qt * TT, kb * 128)
            N = (qt + 1) * TT - q0
            return qt, kb, e, slice(64 * e, 64 * e + 64), q0, N, q0 - qt * TT, q0 - kb * 128

        def D0(it):
            qt, kb, e, rows, q0, N, c0, x0 = geom(it)
            pss = self.ps[5 + it % NB]
            p.mm(pss[:, 0:N], kTz[e][:, kb * 128:(kb + 1) * 128], qT[:, q0:q0 + N])

        def D1(it):
            qt, kb, e, rows, q0, N, c0, x0 = geom(it)
            b = it % NB
            p.act(pt[b][:, 0:N], self.ps[5 + b][:, 0:N], AF.Exp, scale=0.125)

        def D2(it):
            qt, kb, e, rows, q0, N, c0, x0 = geom(it)
            b = it % NB
            p.tt("DVE", pm[b][:, 0:N], pt[b][:, 0:N], mstrip[:, x0:x0 + N], ALU.mult)

        def D3(it):
            qt, kb, e, rows, q0, N, c0, x0 = geom(it)
            b = it % NB
            num, den = self.ps[1 + e], self.ps[3 + e]
            nkb = 4 * qt + 4
            p.mm(num[:, c0:c0 + N], v[:, kb, :], pm[b][:, 0:N], start=(kb == 0), stop=(kb == nkb - 1))
            p.mm(den[:, c0:c0 + N], self.ones_b, pm[b][:, 0:N], start=(kb == 0), stop=(kb == nkb - 1))
            if kb == nkb - 1:
                tsl = slice(qt * TT, (qt + 1) * TT)
                p.act(rden[rows, :], den[rows, :], AF.Ln)
                p.act(rden[rows, :], rden[rows, :], AF.Exp, scale=-1.0)
                p.tt("DVE", self.mix[rows, 4 + pr, tsl], num[rows, :], rden[rows, :], ALU.mult)

        pipeline(len(items), [D3, D2, D1, D0], [3, 2, 1, 0])


def gla(self):
    p = self.p
    a0 = self.arena_off
    l_hi = p.sb("g_lhi", 16 * 256, BF16, off=a0).rearrange("p (b d) -> p b d", b=16)
    l_lo = p.sb("g_llo", 16 * 256, BF16, off=a0 + 8192).rearrange("p (b d) -> p b d", b=16)
    o = a0 + 16384
    arT = p.sb("g_arT", S, BF16, off=o)
    pre = p.sb("g_pre", 256, F32, off=o + 4096)
    lf = p.sb("g_lf", 256, F32, off=o + 5120)
    q_dec = p.sb("g_qdec", S, BF16, off=o)
    k_inv = p.sb("g_kinv", S, BF16, off=o + 4096)
    k_tail = p.sb("g_ktail", 16 * 64, BF16, off=o + 8192).rearrange("p (b d) -> p b d", b=16)
    v = p.sb("g_v", 16 * 128, BF16, off=o + 10240).rearrange("p (b d) -> p b d", b=16)
    e_pos = p.sb("g_epos", TT, F32, off=o + 14336)
    e_neg = p.sb("g_eneg", TT, F32, off=o + 16384)
    sq = p.sb("g_sq", TT, F32, off=o + 14336)
    r = p.sb("g_r", TT, F32, off=o + 16384)
    sgt = p.sb("g_sg", TT, F32, off=o + 18432)
    t1 = p.sb("g_t1", TT, F32, off=o + 20480)
    wgt = [p.sb(f"g_wgt{i}", 64, F32, off=o + 22528 + i * 256) for i in range(2)]
    scm = [p.sb(f"g_scm{i}", 128, BF16, off=o + 23040 + i * 256) for i in range(2)]
    Sf2 = [p.sb(f"g_sf{i}", 128, F32, off=o + 23552 + i * 512) for i in range(2)]
    dec = p.sb("g_dec", 32, F32, off=o + 24576)
    Sall = p.sb("g_sall", 32 * 128, BF16, off=o + 24704).rearrange("p (n d) -> p n d", n=32)
    o_sb = p.sb("g_osb", S, F32, off=self.mix_off + 4 * S * 2)
    w0 = self.wview("w_in0")
    war = self.wload(w0[:, :, A_R:A_R + 16])
    i = self.walloc()
    wa2 = self.wslots[i][0:16, 0:256]
    p.dma("POOL", wa2, self.w["wa2"][0:16, :], f"w{i}")
    for tt in range(NTT):
        ps = self.ps[tt % 2]
        proj_fm(self, war, 16, tt, ps)
        p.act(arT[0:16, tt * TT:(tt + 1) * TT], ps[0:16, :], AF.Copy)
    for tb in range(16):
        ps = self.ps[2 + tb % 2]
        p.mm(ps[:, 0:256], arT[0:16, tb * 128:(tb + 1) * 128], wa2)
        p.tt("DVE", pre, ps[:, 0:256], self.params[:, P_BA:P_BA + 256], ALU.add)
        p.act(pre, pre, AF.Exp, scale=-1.0)
        p.act(lf, pre, AF.Ln, bias=1.0)
        p.act(l_hi[:, tb, :], pre, AF.Ln, bias=1.0)
        p.tt("DVE", l_lo[:, tb, :], lf, l_hi[:, tb, :], ALU.subtract)
    def gla_w(h):
        return (self.wload(w0[:, :, A_Q + h * 64:A_Q + (h + 1) * 64]),
                self.wload(w0[:, :, A_K + h * 64:A_K + (h + 1) * 64]),
                self.wload(w0[:, :, A_V + h * 128:A_V + (h + 1) * 128]),
                self.wload(w0[:, :, A_G + h * 128:A_G + (h + 1) * 128]))
    gnext = gla_w(0)
    for h in range(4):
        hs = slice(h * 64, (h + 1) * 64)
        wq, wk, wv, wg = gnext
        for tt in range(NTT):
            tsl = slice(tt * TT, (tt + 1) * TT)
            pcs = self.ps[3 * (tt % 2)]
            for j in range(4):
                tb = tt * 4 + j
                p.mm(pcs[0:64, j * 128:(j + 1) * 128], l_hi[:, tb, hs], cm(self, 1), start=True, stop=False)
                p.mm(pcs[0:64, j * 128:(j + 1) * 128], l_lo[:, tb, hs], cm(self, 1), start=False, stop=True)
            p.act(e_pos[0:64, :], pcs[0:64, :], AF.Exp, scale=-1.0 / 16)
            p.act(e_neg[0:64, :], pcs[0:64, :], AF.Exp, scale=1.0 / 16)
            pq, pk = self.ps[3 * (tt % 2) + 1], self.ps[3 * (tt % 2) + 2]
            proj_fm(self, wq, 64, tt, pq)
            proj_fm(self, wk, 64, tt, pk)
            p.stt(q_dec[0:64, tsl], pq[0:64, :], 0.125, e_pos[0:64, :], ALU.mult, ALU.mult)
            p.tt("DVE", k_inv[0:64, tsl], pk[0:64, :], e_neg[0:64, :], ALU.mult)
            p.copy("DVE", dec[0:64, tt * 8:(tt + 1) * 8],
                   e_pos[0:64, :].rearrange("p (n c) -> p n c", c=64)[:, :, 63])
        for tb in range(16):
            b = tb % 2
            pkt, pD, pv = (self.ps[3], self.ps[4], self.ps[5]) if tb % 2 == 0 else (self.ps[0], self.ps[1], self.ps[2])
            proj_tm(self, wk, 64, tb, pkt)
            p.mm(pD[:, 0:64], cm(self, 2), l_hi[:, tb, hs], start=True, stop=False)
            p.mm(pD[:, 0:64], cm(self, 2), l_lo[:, tb, hs], start=False, stop=True)
            p.act(wgt[b], pD[:, 0:64], AF.Exp, scale=-1.0 / 16)
            p.tt("DVE", k_tail[:, tb, :], pkt[:, 0:64], wgt[b], ALU.mult)
            proj_tm(self, wv, 128, tb, pv)
            p.act(v[:, tb, :], pv[:, 0:128], AF.Copy)
        if h + 1 < 4:
            gnext = gla_w(h + 1)
        p.memset("DVE", Sf2[0][0:64, :], 0.0)
        for n in range(31):
            tb, j = n // 2, n % 2
            pkv = self.ps[n % 4]
            p.mm(pkv[0:64, 0:128], k_tail[64 * j:64 * j + 64, tb, :], v[64 * j:64 * j + 64, tb, :])
            p.stt(Sf2[(n + 1) % 2][0:64, :], Sf2[n % 2][0:64, :], dec[0:64, n:n + 1], pkv[0:64, 0:128],
                  ALU.mult, ALU.add)
            p.act(Sall[0:64, n + 1, :], Sf2[(n + 1) % 2][0:64, :], AF.Copy)
        for tb in range(16):
            b = tb % 2
            bsl = slice(tb * 128, (tb + 1) * 128)
            psc, pso = self.ps[4 + b], self.ps[6 + b]
            p.mm(psc[:, 0:128], k_inv[0:64, bsl], q_dec[0:64, bsl])
            p.tt("DVE", scm[b], psc[:, 0:128], cm(self, 1), ALU.mult)
            p.mm(pso[:, 0:128], v[:, tb, :], scm[b], start=True, stop=False)
            for j in range(2):
                n = 2 * tb + j
                csl = slice(tb * 128 + 64 * j, tb * 128 + 64 * j + 64)
                if n > 0:
                    p.mm(pso[:, 64 * j:64 * j + 64], Sall[0:64, n, :], q_dec[0:64, csl],
                         start=False, stop=(j == 1))
            p.act(o_sb[:, bsl], pso[:, 0:128], AF.Copy)
        r_all = p.sb(f"g_rall{h}", 4 * TT, F32, off=o).rearrange("p (a t) -> p a t", a=4)
        for tt in range(NTT):
            tsl = slice(tt * TT, (tt + 1) * TT)
            pss = self.ps[2 + tt % 2]
            p.act(sq, o_sb[:, tsl], AF.Square)
            p.mm(pss, self.ones_f, sq)
            p.act(r_all[:, tt, :], pss, AF.Ln, scale=1.0 / 128, bias=EPS)
            p.act(r_all[:, tt, :], r_all[:, tt, :], AF.Exp, scale=-0.5)
        for tt in range(NTT):
            tsl = slice(tt * TT, (tt + 1) * TT)
            pg = self.ps[4 + tt % 2]
            proj_fm(self, wg, 128, tt, pg)
            p.act(sgt, pg, AF.Silu)
            p.stt(t1, o_sb[:, tsl], self.pcol(P_GLAN), r_all[:, tt, :], ALU.mult, ALU.mult)
            p.tt("DVE", self.mix[:, h, tsl], t1, sgt, ALU.mult)


def mixer0(self):
    self.rmsnorm(P_NMIX0)
    gla(self)
    dsw(self)
    out_proj(self, "w_out0")


def conformer(self):
    p = self.p
    a0 = self.arena_off
    cbuf = p.sb("c_cbuf", 4 * 542, BF16, off=a0).rearrange("p (c t) -> p c t", c=4)
    o = a0 + 4352
    ysb = p.sb("c_ysb", 4 * TT, F32, off=o).rearrange("p (c t) -> p c t", c=4)
    sig = p.sb("c_sig", TT, F32, off=o + 8192)
    sq = p.sb("c_sq", TT, F32, off=o + 10240)
    mean = p.sb("c_mean", TT, F32, off=o + 12288)
    msq = p.sb("c_msq", TT, F32, off=o + 14336)
    r = p.sb("c_r", TT, F32, off=o + 16384)
    dg = [p.sb(f"c_dg{i}", 31 * 128, BF16, off=o + 18432 + i * 7936) for i in range(2)]
    w1 = self.wview("w_in1")
    wca = [self.wload(w1[:, :, C_A + c * 128:C_A + (c + 1) * 128]) for c in range(4)]
    wcb = [self.wload(w1[:, :, C_B + c * 128:C_B + (c + 1) * 128]) for c in range(4)]
    def build_dg(dgb, c):
        for k in range(31):
            p.ts("DVE", dgb[:, k * 128:(k + 1) * 128], cm(self, 0), self.pcol(P_CW + c * 31 + k), ALU.mult)

    sigs = [sig, p.sb("c_sig2", TT, F32, off=o + 18432 + 2 * 7936)]
    build_dg(dg[0], 0)

    def C0(it):
        tt, c = it // 4, it % 4
        pa, pb = (self.ps[0], self.ps[1]) if it % 2 == 0 else (self.ps[6], self.ps[7])
        sg_ = sigs[it % 2]
        proj_fm(self, wca[c], 128, tt, pa)
        proj_fm(self, wcb[c], 128, tt, pb)
        p.act(sg_, pb, AF.Sigmoid)
        if tt == 0:
            p.memset("POOL", cbuf[:, c, 0:30], 0.0)
        else:
            p.copy("POOL", cbuf[:, c, 0:30], cbuf[:, c, 512:542])
        p.tt("DVE", cbuf[:, c, 30:542], pa, sg_, ALU.mult)

    def C1(it):
        tt, c = it // 4, it % 4
        tsl = slice(tt * TT, (tt + 1) * TT)
        psum_s, psum_q = self.ps[4], self.ps[5]
        dgb = dg[it % 2]
        if it + 1 < 16:
            build_dg(dg[(it + 1) % 2], (it + 1) % 4)
        py = self.ps[2 + c % 2]
        for k in range(31):
            p.mm(py, dgb[:, k * 128:(k + 1) * 128], cbuf[:, c, k:k + 512], start=(k == 0), stop=(k == 30))
        p.ts("DVE", ysb[:, c, :], py, self.pcol(P_CB + c), ALU.add)
        p.mm(psum_s, self.ones_f, ysb[:, c, :], start=(c == 0), stop=(c == 3))
        p.act(sq, ysb[:, c, :], AF.Square)
        p.mm(psum_q, self.ones_f, sq, start=(c == 0), stop=(c == 3))
        if c == 3:
            p.ts("DVE", mean, psum_s, 1.0 / 512, ALU.mult)
            p.tt("DVE", msq, mean, mean, ALU.mult)
            p.stt(r, psum_q, 1.0 / 512, msq, ALU.mult, ALU.subtract)
            p.act(r, r, AF.Ln, bias=EPS)
            p.act(r, r, AF.Exp, scale=-0.5)
            for cc in range(4):
                p.tt("DVE", ysb[:, cc, :], ysb[:, cc, :], mean, ALU.subtract)
                p.tt("DVE", ysb[:, cc, :], ysb[:, cc, :], r, ALU.mult)
                p.act(self.mix[:, cc, tsl], ysb[:, cc, :], AF.Silu, scale=self.pcol(P_LG + cc),
                      bias=self.pcol(P_LB + cc))

    pipeline(16, [C0, C1], [0, 1])


def pipeline(n, stages, skews):
    mx = max(skews)
    for step in range(n + mx):
        for f, sk in zip(stages, skews):
            it = step - sk
            if 0 <= it < n:
                f(it)


def stickbreak(self):
    p = self.p
    a0 = self.arena_off
    qT = p.sb("s_qT", S, BF16, off=a0)
    kTz = [p.sb(f"s_kTz{e}", S, BF16, off=a0 + 4096 + e * 4096) for e in range(2)]
    v = p.sb("s_v", S, BF16, off=a0 + 12288).rearrange("p (b d) -> p b d", b=16)
    o = a0 + 16384
    p.memset("POOL", kTz[0][64:128, :], 0.0)
    p.memset("POOL", kTz[1][0:64, :], 0.0)
    NB = 3
    E = [p.sb(f"s_E{i}", TT, F32, off=o + i * 2048) for i in range(NB)]
    SPb = [p.sb(f"s_SP{i}", TT, BF16, off=o + 6144 + i * 1024) for i in range(NB)]
    R = [p.sb(f"s_R{i}", TT, F32, off=o + 9216 + i * 2048) for i in range(2)]
    arg = [p.sb(f"s_arg{i}", TT, F32, off=o + 13312 + i * 2048) for i in range(NB)]
    at = [p.sb(f"s_a{i}", TT, BF16, off=o + 19456 + i * 1024) for i in range(NB)]
    zeros_b = p.sb("s_zeros", 128, BF16, off=o + 22528)
    p.memset("DVE", zeros_b, 0.0)
    w1 = self.wview("w_in1")
    def sb_w(pr):
        return (self.wload(w1[:, :, D_Q + pr * 128:D_Q + (pr + 1) * 128]),
                self.wload(w1[:, :, D_K + pr * 128:D_K + (pr + 1) * 128]),
                self.wload(w1[:, :, D_V + pr * 128:D_V + (pr + 1) * 128]))
    snext = sb_w(0)
    for pr in range(4):
        wq, wk, wv = snext
        for (wa, dst) in ((wq, qT), (wk, None)):
            for tt in range(NTT):
                ps = self.ps[tt % 2]
                tsl = slice(tt * TT, (tt + 1) * TT)
                proj_fm(self, wa, 128, tt, ps)
                if dst is not None:
                    p.act(dst[:, tsl], ps, AF.Copy)
                else:
                    p.act(kTz[0][0:64, tsl], ps[0:64, :], AF.Copy)
                    p.act(kTz[1][64:128, tsl], ps[64:128, :], AF.Copy)
        for tb in range(16):
            ps = self.ps[tb % 2]
            proj_tm(self, wv, 128, tb, ps)
            p.act(v[:, tb, :], ps[:, 0:128], AF.Copy)
        if pr + 1 < 4:
            snext = sb_w(pr + 1)
        items = []
        for qt in range(4):
            for e in range(2):
                for kb in range(4 * qt + 3, -1, -1):
                    items.append((qt, kb, e))

        def geom(it):
            qt, kb, e = items[it]
            q0 = max(qt * TT, kb * 128)
            N = (qt + 1) * TT - q0
            return qt, kb, e, slice(64 * e, 64 * e + 64), q0, N, q0 - qt * TT, kb >= 4 * qt

        def S0(it):
            qt, kb, e, rows, q0, N, c0, diag = geom(it)
            pz = self.ps[it % 4]
            p.mm(pz[:, 0:N], kTz[e][:, kb * 128:(kb + 1) * 128], qT[:, q0:q0 + N], start=True, stop=not diag)
            if diag:
                p.mm(pz[:, 0:128], cm(self, 0), cm(self, 7), start=False, stop=True)

        def S1(it):
            qt, kb, e, rows, q0, N, c0, diag = geom(it)
            b = it % NB
            pz = self.ps[it % 4]
            p.act(E[b][:, 0:N], pz[:, 0:N], AF.Exp, scale=0.125)
            p.act(SPb[b][:, 0:N], E[b][:, 0:N], AF.Ln, bias=1.0)

        def S2(it):
            qt, kb, e, rows, q0, N, c0, diag = geom(it)
            b = it % NB
            pz, prr = self.ps[it % 4], self.ps[4 + it % 2]
            p.mm(pz[:, 0:N], cm(self, 4), SPb[b][:, 0:N], start=False, stop=True, skip=True)
            p.mm(prr[:, 0:N], cm(self, 6), SPb[b][:, 0:N])

        def S3(it):
            qt, kb, e, rows, q0, N, c0, diag = geom(it)
            b = it % NB
            pz, prr = self.ps[it % 4], self.ps[4 + it % 2]
            if kb == 4 * qt + 3:
                p.memset("POOL", R[e], 0.0)
            p.stt(arg[b][:, 0:N], pz[:, 0:N], 0.125, R[e][:, c0:c0 + N], ALU.mult, ALU.subtract)
            p.tt("DVE", R[e][:, c0:c0 + N], R[e][:, c0:c0 + N], prr[:, 0:N], ALU.add)

        def S4(it):
            qt, kb, e, rows, q0, N, c0, diag = geom(it)
            p.act(at[it % NB][:, 0:N], arg[it % NB][:, 0:N], AF.Exp)

        def S5(it):
            qt, kb, e, rows, q0, N, c0, diag = geom(it)
            po = self.ps[6 + e]
            if kb == 4 * qt + 3:
                p.mm(po, zeros_b, self.cmask[:, 0:512], start=True, stop=False)
            p.mm(po[:, c0:c0 + N], v[:, kb, :], at[it % NB][:, 0:N], start=False, stop=(kb == 0))
            if kb == 0:
                p.act(self.mix[rows, 4 + pr, qt * TT:(qt + 1) * TT], po[rows, :], AF.Copy)

        pipeline(len(items), [S5, S4, S3, S2, S1, S0], [5, 4, 3, 2, 1, 0])


def mixer1(self):
    self.rmsnorm(P_NMIX1)
    if "noconf" not in self.stages:
        conformer(self)
    if "nosb" not in self.stages:
        stickbreak(self)
    out_proj(self, "w_out1")


import numpy as np

def pcols(v):
    v = np.asarray(v, np.float32)
    return np.ascontiguousarray(v.reshape(-1, 128).T)

def host_params(I):
    P = np.zeros((128, NPAR), np.float32)
    P[:, P_NMIX0:P_NMIX0 + 8] = pcols(I["norm_mix0"])
    P[:, P_NFFN0:P_NFFN0 + 8] = pcols(I["norm_ffn0"])
    P[:, P_NMIX1:P_NMIX1 + 8] = pcols(I["norm_mix1"])
    P[:, P_NFFN1:P_NFFN1 + 8] = pcols(I["norm_ffn1"])
    P[:, P_NFIN:P_NFIN + 8] = pcols(I["final_norm"])
    P[:, P_GLAN] = np.asarray(I["gla_norm"], np.float32)
    for L, off in ((0, P_FC0), (1, P_FC1)):
        fc = np.asarray(I[f"ffn_conv{L}"], np.float32)
        a = fc.reshape(3, 44, 128).transpose(2, 1, 0)
        P[:, off:off + 132] = a.reshape(128, 132)
    cw = np.asarray(I["conv_w1"], np.float32)
    P[:, P_CW:P_CW + 124] = cw.reshape(31, 4, 128).transpose(2, 1, 0).reshape(128, 124)
    P[:, P_CB:P_CB + 4] = pcols(I["conv_b1"])
    P[:, P_LG:P_LG + 4] = pcols(I["conv_ln_g1"])
    P[:, P_LB:P_LB + 4] = pcols(I["conv_ln_b1"])
    P[:, P_BA:P_BA + 256] = np.asarray(I["gla_ba"], np.float32)[None, :]
    return P

def host_consts():
    C = {}
    half = 8
    inv = (np.float32(500000.0) ** (-np.arange(half, dtype=np.float32) / np.float32(half))).astype(np.float32)
    ang = (np.arange(S, dtype=np.float32)[:, None] * inv[None, :]).astype(np.float32)
    cos, sin = np.cos(ang).astype(np.float32), np.sin(ang).astype(np.float32)
    cosT = np.ones((128, S), np.float32)
    sinT = np.zeros((128, S), np.float32)
    for base in (0, 64):
        cosT[base:base + 8] = cos.T
        cosT[base + 8:base + 16] = cos.T
        sinT[base:base + 8] = -sin.T
        sinT[base + 8:base + 16] = sin.T
    C["cosT"], C["sinT"] = cosT, sinT
    ki = np.arange(128)[:, None]
    x = np.arange(2176)[None, :]
    d = x - ki
    c = ((d >= 0) & (d <= 128)).astype(np.float32) + ((d >= 0) & (d % 4 == 0) & (d <= 512)) + ((d >= 0) & (d % 16 == 0) & (d <= 2048))
    C["mstrip"] = c.astype(np.float32)
    cm = np.zeros((128, 8, 128), np.float32)
    i = np.arange(128)[:, None]
    j = np.arange(128)[None, :]
    cm[:, 0] = (i == j)
    cm[:, 1] = ((i // 64) == (j // 64)) & (i <= j)
    cm[:, 2] = ((i // 64) == (j // 64)) & (i > j)
    cm[:, 3] = ((i // 64) == (j // 64)) & (i <= j)
    cm[:, 4] = np.where(i >= j, -8.0, 0.0)
    cm[:, 5] = (i >= j)
    cm[:, 6] = 1.0
    cm[:, 7] = np.where(i >= j, -240000.0, 0.0)
    C["cmask"] = cm.reshape(128, 1024)
    return C

def host_inmap(I, seqs):
    x = np.asarray(I["x"], np.float32)
    m = {"xT": np.ascontiguousarray(x[seqs].transpose(0, 2, 1)), "params": host_params(I)}
    for n in ["w_in0", "w_out0", "ffn_up0", "ffn_down0", "w_in1", "w_out1", "ffn_up1", "ffn_down1"]:
        m[n] = np.ascontiguousarray(np.asarray(I[n], np.float32))
    w = m["w_in0"]
    GO = 1552
    def rot(base):
        cols = []
        for h in range(8):
            b = base + h * 64
            idx = np.arange(b, b + 64)
            idx[0:8] = np.arange(b + 8, b + 16)
            idx[8:16] = np.arange(b, b + 8)
            cols.append(idx)
        return np.concatenate(cols)
    m["wrot0"] = np.ascontiguousarray(np.concatenate([w[:, rot(GO)], w[:, rot(GO + 512)]], axis=1))
    wa2 = np.zeros((32, 256), np.float32)
    wa2[:16] = np.asarray(I["gla_wa2"], np.float32)
    m["wa2"] = wa2
    m.update(host_consts())
    return m


def kernel(**inputs):
    I = {k: np.asarray(v) for k, v in inputs.items()}
    B = I["x"].shape[0]
    ncores = 8
    per = B // ncores
    k = build(per, ["mix0", "ffn0", "mix1", "ffn1", "final"])
    base = host_inmap(I, list(range(per)))
    x = np.asarray(I["x"], np.float32)
    in_maps = []
    for c in range(ncores):
        m = dict(base)
        m["xT"] = np.ascontiguousarray(x[c * per:(c + 1) * per].transpose(0, 2, 1))
        in_maps.append(m)
    res = run_bass_kernel_spmd(k.p.nc, in_maps, core_ids=list(range(ncores)))
    out = np.empty((B, S, D), np.float32)
    for c in range(ncores):
        yT = np.asarray(res.results[c]["yT"])
        out[c * per:(c + 1) * per] = yT.transpose(0, 2, 1)
    return out
```

```python
import numpy as np
import concourse.bass as bass
import concourse.mybir as mybir
from concourse.bass_utils import run_bass_kernel_spmd

F32 = mybir.dt.float32
BF16 = mybir.dt.bfloat16
AF = mybir.ActivationFunctionType
ALU = mybir.AluOpType
ESZ = {F32: 4, BF16: 2}

ENGS = ("PE", "ACT", "DVE", "POOL", "SP")


class Prog:
    def __init__(self):
        self.nc = bass.Bass("TRN2", target_bir_lowering=False)
        nc = self.nc
        self.eng = {"PE": nc.tensor, "ACT": nc.scalar, "DVE": nc.vector,
                    "POOL": nc.gpsimd, "SP": nc.sync}
        self.sem = {}
        for e in ENGS:
            self.sem[e] = nc.alloc_semaphore("s_" + e)
        self.cnt = {e: 0 for e in ENGS}
        self.dcnt = {}
        self.waited = {e: {} for e in ENGS}
        self.base = {}
        cap = 4096
        self.cap = cap
        self.rec = np.zeros((cap, 6), dtype=np.int64)
        self.rec_ev = [None] * cap
        self.alive = np.zeros(cap, dtype=bool)
        self.nrec = 0
        self.index = {}
        self.sb_off = 16384
        self.n_ps = 0
        self.n_inst = 0
        self.n_wait = 0
        self.attach_waits = True
        self._attach = None

    def sb(self, name, free, dtype, off=None):
        nbytes = free * ESZ[dtype]
        if off is None:
            off = self.sb_off
            self.sb_off = off + ((nbytes + 31) // 32) * 32
        assert off + nbytes <= 229368, (name, off, nbytes)
        t = self.nc.alloc_sbuf_tensor_at(name, [128, free], dtype, offset=off)
        self.base[t.name] = (0, off)
        return t.ap()

    def psum(self, name):
        t = self.nc.alloc_psum_tensor(name, [128, 512], F32)
        self.base[t.name] = (1, self.n_ps * 2048)
        self.n_ps += 1
        return t.ap()

    def dsem(self, name):
        if name not in self.sem:
            self.sem[name] = self.nc.alloc_semaphore("d_" + name)
            self.dcnt[name] = 0
        return name

    def _rect(self, ap):
        tn = ap.tensor.name
        if tn not in self.base:
            return None
        space, base = self.base[tn]
        pat = ap.ap
        esz = ESZ[ap.dtype]
        F = 1
        for s in list(ap.tensor.shape)[1:]:
            F *= int(s)
        off = int(ap.offset)
        p0 = off // F
        f0 = off - p0 * F
        pcnt = int(pat[0][1])
        ext = 0
        for st, c in pat[1:]:
            assert st >= 0
            ext += int(st) * (int(c) - 1)
        assert f0 + ext < F, (tn, f0, ext, F)
        b0, b1 = base + f0 * esz, base + (f0 + ext + 1) * esz
        if space == 1:
            b0 = (b0 // 2048) * 2048
            b1 = ((b1 + 2047) // 2048) * 2048
        return (space, p0, p0 + pcnt, b0, b1)

    def _overlaps(self, r):
        n = self.nrec
        if n == 0:
            return []
        R = self.rec[:n]
        m = (self.alive[:n] & (R[:, 0] == r[0]) & (R[:, 1] < r[2]) & (R[:, 2] > r[1])
             & (R[:, 3] < r[4]) & (R[:, 4] > r[3]))
        return np.nonzero(m)[0]

    def _add_rec(self, r, isw, ev):
        key = (r, isw, ev[0])
        i = self.index.get(key)
        if i is not None and self.alive[i]:
            if self.rec_ev[i][1] < ev[1]:
                self.rec_ev[i] = ev
            return
        if self.nrec >= self.cap:
            self._compact()
        i = self.nrec
        self.nrec += 1
        self.rec[i] = (r[0], r[1], r[2], r[3], r[4], isw)
        self.rec_ev[i] = ev
        self.alive[i] = True
        self.index[key] = i

    def _compact(self):
        n = self.nrec
        keep = np.nonzero(self.alive[:n])[0]
        if len(keep) > self.cap // 2:
            newcap = self.cap * 2
            rec = np.zeros((newcap, 6), dtype=np.int64)
            rec[:n] = self.rec[:n]
            self.rec = rec
            self.rec_ev = self.rec_ev + [None] * (newcap - self.cap)
            al = np.zeros(newcap, dtype=bool)
            al[:n] = self.alive[:n]
            self.alive = al
            self.cap = newcap
        evs = [self.rec_ev[i] for i in keep]
        k = len(keep)
        self.rec[:k] = self.rec[keep]
        self.alive[:] = False
        self.alive[:k] = True
        for j in range(k):
            self.rec_ev[j] = evs[j]
        self.nrec = k
        self.index = {}
        for j in range(k):
            r = tuple(int(x) for x in self.rec[j, :5])
            self.index[(r, int(self.rec[j, 5]), self.rec_ev[j][0])] = j

    def _sync(self, eng, outs, ins, is_dma=False):
        need = {}

        def add(ev, raw):
            k, v = ev
            if k == eng and not is_dma:
                if eng == "PE" or not raw:
                    return
            if need.get(k, 0) < v:
                need[k] = v

        rin = [self._rect(a) for a in ins]
        rout = [self._rect(a) for a in outs]
        for r in rin:
            if r is None:
                continue
            for i in self._overlaps(r):
                if self.rec[i, 5]:
                    add(self.rec_ev[i], True)
                elif r[0] == 1 and self.rec_ev[i][0] != eng:
                    add(self.rec_ev[i], True)
        for r in rout:
            if r is None:
                continue
            for i in self._overlaps(r):
                add(self.rec_ev[i], True)
        w = self.waited[eng]
        todo = [(k, v) for k, v in need.items() if w.get(k, 0) < v]
        attach = None
        if todo and not is_dma and self.attach_waits:
            attach = todo.pop()
        for k, v in todo:
            self.eng[eng].wait_ge(self.sem[k], v)
            self.n_wait += 1
            w[k] = v
        if attach is not None:
            w[attach[0]] = attach[1]
        self._attach = attach
        return rin, rout

    def _commit(self, rin, rout, ev):
        for r in rout:
            if r is None:
                continue
            n = self.nrec
            R = self.rec[:n]
            m = (self.alive[:n] & (R[:, 0] == r[0]) & (R[:, 1] >= r[1]) & (R[:, 2] <= r[2])
                 & (R[:, 3] >= r[3]) & (R[:, 4] <= r[4]))
            self.alive[:n][m] = False
            self._add_rec(r, 1, ev)
        for r in rin:
            if r is None:
                continue
            self._add_rec(r, 0, ev)

    def emit(self, eng, fn, outs, ins, inc=True):
        rin, rout = self._sync(eng, outs, ins)
        ins_obj = fn(self.eng[eng])
        if self._attach is not None:
            ins_obj._wait_ge(self.sem[self._attach[0]], self._attach[1])
        if inc:
            self.cnt[eng] += 1
            ins_obj.then_inc(self.sem[eng], 1)
            self._commit(rin, rout, (eng, self.cnt[eng]))
        else:
            self._commit(rin, rout, (eng, self.cnt[eng] + 1))
        self.n_inst += 1
        return ins_obj

    def dma(self, q, out, in_, sem, **kw):
        self.dsem(sem)
        rin, rout = self._sync(q, [out], [in_], is_dma=True)
        i = self.eng[q].dma_start(out=out, in_=in_, **kw)
        self.dcnt[sem] += 16
        i.then_inc(self.sem[sem], 16)
        self._commit(rin, rout, (sem, self.dcnt[sem]))
        self.n_inst += 1

    def dma_group(self, q, pairs, sem, **kw):
        self.dsem(sem)
        recs = []
        for out, in_ in pairs:
            rin, rout = self._sync(q, [out], [in_], is_dma=True)
            i = self.eng[q].dma_start(out=out, in_=in_, **kw)
            self.dcnt[sem] += 16
            i.then_inc(self.sem[sem], 16)
            recs.append((rin, rout))
            self.n_inst += 1
        for rin, rout in recs:
            self._commit(rin, rout, (sem, self.dcnt[sem]))

    def barrier(self):
        for e in ENGS:
            w = self.waited[e]
            for k in list(self.sem.keys()):
                v = self.cnt[k] if k in self.cnt else self.dcnt[k]
                if k == e or v == 0 or w.get(k, 0) >= v:
                    continue
                self.eng[e].wait_ge(self.sem[k], v)
                w[k] = v
        self.alive[:] = False
        self.nrec = 0
        self.index = {}

    def finish(self, dma_sems):
        for s in dma_sems:
            if self.waited["SP"].get(s, 0) < self.dcnt[s]:
                self.eng["SP"].wait_ge(self.sem[s], self.dcnt[s])

    def mm(self, out, lhsT, rhs, start=True, stop=True, skip=False):
        return self.emit("PE", lambda e: e.matmul(out, lhsT, rhs, start=start, stop=stop,
                                                  skip_group_check=skip),
                         [out], [lhsT, rhs])

    def act(self, out, in_, func, scale=1.0, bias=None, eng="ACT"):
        ins = [in_]
        kw = {}
        if isinstance(scale, (int, float)):
            kw["scale"] = float(scale)
        else:
            kw["scale"] = scale
            ins.append(scale)
        if bias is not None:
            kw["bias"] = bias
            if not isinstance(bias, (int, float)):
                ins.append(bias)
        return self.emit("ACT", lambda e: e.activation(out=out, in_=in_, func=func, **kw),
                         [out], ins)

    def tt(self, eng, out, in0, in1, op):
        return self.emit(eng, lambda e: e.tensor_tensor(out=out, in0=in0, in1=in1, op=op),
                         [out], [in0, in1])

    def ts(self, eng, out, in0, s1, op0, s2=None, op1=None):
        ins = [in0]
        if not isinstance(s1, (int, float)):
            ins.append(s1)
        if s2 is not None and not isinstance(s2, (int, float)):
            ins.append(s2)
        if op1 is None:
            return self.emit(eng, lambda e: e.tensor_scalar(out=out, in0=in0, scalar1=s1, scalar2=None,
                                                            op0=op0), [out], ins)
        return self.emit(eng, lambda e: e.tensor_scalar(out=out, in0=in0, scalar1=s1, scalar2=s2,
                                                        op0=op0, op1=op1), [out], ins)

    def stt(self, out, in0, scalar, in1, op0, op1):
        ins = [in0, in1]
        if not isinstance(scalar, (int, float)):
            ins.append(scalar)
        return self.emit("DVE", lambda e: e.scalar_tensor_tensor(out=out, in0=in0, scalar=scalar, in1=in1,
                                                                 op0=op0, op1=op1), [out], ins)

    def copy(self, eng, out, in_):
        if eng == "ACT":
            return self.act(out, in_, AF.Copy)
        return self.emit(eng, lambda e: e.tensor_copy(out=out, in_=in_), [out], [in_])

    def memset(self, eng, ap, val):
        return self.emit(eng, lambda e: e.memset(ap, val), [ap], [])

    def recip(self, out, in_):
        return self.emit("DVE", lambda e: e.reciprocal(out=out, in_=in_), [out], [in_])


S = 2048
D = 1024
NTT = 4
TT = 512
DFF = 2816
NFC = 22
EPS = 1e-6

P_NMIX0, P_NFFN0, P_NMIX1, P_NFFN1, P_NFIN, P_GLAN = 0, 8, 16, 24, 32, 40
P_FC0, P_FC1 = 48, 180
P_CW, P_CB, P_LG, P_LB = 312, 436, 440, 444
P_BA = 448
NPAR = 704

NSLOT = 12


def pipeline(n, stages, skews):
    mx = max(skews)
    for step in range(n + mx):
        for f, sk in zip(stages, skews):
            it = step - sk
            if 0 <= it < n:
                f(it)


class K:
    def __init__(self, nseq, stages):
        self.p = Prog()
        self.nseq = nseq
        self.stages = stages
        p = self.p
        nc = p.nc
        dt = lambda name, shape, kind="ExternalInput": nc.dram_tensor(name, shape, F32, kind=kind).ap()
        self.xT = dt("xT", [nseq, D, S])
        self.yT = dt("yT", [nseq, D, S], "ExternalOutput")
        self.params_d = dt("params", [128, NPAR])
        self.w = {}
        for name, shape in [("w_in0", [D, 3088]), ("wrot0", [D, 1024]), ("w_out0", [D, D]),
                            ("ffn_up0", [D, 2 * DFF]), ("ffn_down0", [DFF, D]),
                            ("w_in1", [D, 2560]), ("w_out1", [D, D]),
                            ("ffn_up1", [D, 2 * DFF]), ("ffn_down1", [DFF, D]),
                            ("wa2", [32, 256]), ("cosT", [128, S]), ("sinT", [128, S]),
                            ("mstrip", [128, 2176]), ("cmask", [128, 1024])]:
            self.w[name] = dt(name, shape)

        self.hn_off = p.sb_off + 8 * S * 4
        self.h = p.sb("h", 8 * S, F32).rearrange("p (c t) -> p c t", c=8)
        self.hn = p.sb("hn", 8 * S, BF16).rearrange("p (c t) -> p c t", c=8)
        self.mix_off = p.sb_off
        self.mix = p.sb("mix", 8 * S, BF16).rearrange("p (c t) -> p c t", c=8)
        self.params = p.sb("params", NPAR, F32)
        self.ones_f = p.sb("ones_f", 128, F32)
        self.ones_b = p.sb("ones_b", 128, BF16)
        self.cmask = p.sb("cmaskb", 1024, BF16)
        self.wslots = [p.sb(f"wslot{i}", 1024, BF16) for i in range(NSLOT)]
        self.wnext = 0
        self.arena_off = p.sb_off
        self.ps = [p.psum(f"ps{i}") for i in range(8)]
        print("persistent SBUF bytes", p.sb_off)

    def pcol(self, c, n=1):
        return self.params[:, c:c + n]

    def walloc(self):
        i = self.wnext
        self.wnext = (i + 1) % NSLOT
        return i

    def wload(self, dram_ap, shape3=None):
        i = self.walloc()
        a, b = dram_ap.shape[1], dram_ap.shape[2]
        dst = self.wslots[i][:, 0:a * b].rearrange("p (a b) -> p a b", a=a)
        self.p.dma("POOL", dst, dram_ap, f"w{i}")
        return dst

    def wview(self, name):
        return self.w[name].rearrange("(c p) n -> p c n", p=128)

    def prologue(self):
        p = self.p
        p.dma("SP", self.params, self.params_d, "par")
        p.memset("DVE", self.ones_f, 1.0)
        p.memset("DVE", self.ones_b, 1.0)
        p.dma("POOL", self.cmask, self.w["cmask"], "cm")
        self.ident_b = self.cmask[:, 0:128]

    def load_x(self, s):
        for tt in range(NTT):
            tsl = slice(tt * TT, (tt + 1) * TT)
            self.p.dma_group("SP", [(self.h[:, c, tsl], self.xT[s, c * 128:(c + 1) * 128, tsl]) for c in range(8)],
                             f"x{tt}")

    def store_y(self, s, src):
        for tt in range(NTT):
            tsl = slice(tt * TT, (tt + 1) * TT)
            self.p.dma_group("SP", [(self.yT[s, c * 128:(c + 1) * 128, tsl], src[:, c, tsl]) for c in range(8)],
                             f"y{tt}")

    def rmsnorm(self, gcol, out_bf16=True, out=None):
        p = self.p
        out = self.hn if out is None else out
        sq = p.sb("rn_sq", 2 * TT, BF16, off=self.arena_off).rearrange("p (b t) -> p b t", b=2)
        rstd = p.sb("rn_rstd", 2 * TT, F32, off=self.arena_off + 2 * TT * 4).rearrange("p (b t) -> p b t", b=2)
        for tt in range(NTT):
            tsl = slice(tt * TT, (tt + 1) * TT)
            ps = self.ps[tt % 2]
            for c in range(8):
                b = c % 2
                p.act(sq[:, b, :], self.h[:, c, tsl], AF.Square)
                p.mm(ps, self.ones_b, sq[:, b, :], start=(c == 0), stop=(c == 7))
            r = rstd[:, tt % 2, :]
            p.act(r, ps, AF.Ln, scale=1.0 / D, bias=EPS)
            p.act(r, r, AF.Exp, scale=-0.5)
            for c in range(8):
                p.stt(out[:, c, tsl], self.h[:, c, tsl], self.pcol(gcol + c), r, ALU.mult, ALU.mult)

    def ffn(self, L):
        p = self.p
        up = self.wview(f"ffn_up{L}")
        down = self.w[f"ffn_down{L}"]
        fc = P_FC0 if L == 0 else P_FC1
        self.rmsnorm(P_NFFN0 if L == 0 else P_NFFN1)
        a0 = self.arena_off
        NB = 3
        cg = [p.sb(f"f_cg{i}", TT, F32, off=a0 + i * 2048) for i in range(NB)]
        sg = [p.sb(f"f_sg{i}", TT, F32, off=a0 + 6144 + i * 2048) for i in range(NB)]
        cv = [p.sb(f"f_cv{i}", TT, F32, off=a0 + 12288 + i * 2048) for i in range(NB)]
        b0 = [p.sb(f"f_b0{i}", TT + 2, F32, off=a0 + 18432 + i * 2080) for i in range(NB)]
        b1 = [p.sb(f"f_b1{i}", TT + 2, F32, off=a0 + 24672 + i * 2080) for i in range(NB)]
        b2 = [p.sb(f"f_b2{i}", TT, F32, off=a0 + 30912 + i * 2048) for i in range(NB)]
        hg = [p.sb(f"f_hg{i}", 2, F32, off=a0 + 37056 + i * 32) for i in range(2)]
        act = self.mix
        groups = [list(range(0, 8)), list(range(8, 16)), list(range(16, 22))]
        gbase = 0
        for grp in groups:
            items = [(j, ci, tt) for j, ci in enumerate(grp) for tt in range(NTT)]
            wq = {}

            def need(ci):
                if ci not in wq and ci in grp:
                    wq[ci] = (self.wload(up[:, :, ci * 128:(ci + 1) * 128]),
                              self.wload(up[:, :, DFF + ci * 128:DFF + (ci + 1) * 128]))

            def P1(it, gbase=gbase):
                j, ci, tt = items[it]
                g = gbase + it
                b = g % NB
                if tt == 0:
                    need(ci)
                    if j + 1 < len(grp):
                        need(grp[j + 1])
                wg, wv = wq[ci]
                tsl = slice(tt * TT, (tt + 1) * TT)
                pg, pv = self.ps[b], self.ps[3 + b]
                w0g, w1g, w2g = (self.pcol(fc + ci * 3 + k) for k in range(3))
                w0v, w1v, w2v = (self.pcol(fc + (NFC + ci) * 3 + k) for k in range(3))
                for k in range(8):
                    p.mm(pg, wg[:, k, :], self.hn[:, k, tsl], start=(k == 0), stop=(k == 7))
                for k in range(8):
                    p.mm(pv, wv[:, k, :], self.hn[:, k, tsl], start=(k == 0), stop=(k == 7))
                p.act(b0[b][:, 2:TT + 2], pv, AF.Copy, scale=w0v)
                p.act(b1[b][:, 1:TT + 1], pv, AF.Copy, scale=w1v)
                p.act(b2[b], pv, AF.Copy, scale=w2v)
                c_ = cg[b]
                hprev = hg[(tt + 1) % 2]
                if tt == 0:
                    p.memset("DVE", c_[:, 0:2], 0.0)
                else:
                    p.ts("DVE", c_[:, 0:2], hprev[:, 0:2], w0g, ALU.mult)
                    p.stt(c_[:, 0:1], hprev[:, 1:2], w1g, c_[:, 0:1], ALU.mult, ALU.add)
                p.ts("DVE", c_[:, 2:TT], pg[:, 0:TT - 2], w0g, ALU.mult)
                p.stt(c_[:, 1:TT], pg[:, 0:TT - 1], w1g, c_[:, 1:TT], ALU.mult, ALU.add)
                p.stt(c_[:, 0:TT], pg[:, 0:TT], w2g, c_[:, 0:TT], ALU.mult, ALU.add)
                if tt < NTT - 1:
                    p.copy("DVE", hg[tt % 2][:, 0:2], pg[:, TT - 2:TT])

            def P2(it, gbase=gbase):
                j, ci, tt = items[it]
                g = gbase + it
                b = g % NB
                bp = (g - 1) % NB
                p.act(sg[b], cg[b], AF.Silu)
                if tt == 0:
                    p.memset("POOL", b0[b][:, 0:2], 0.0)
                    p.memset("POOL", b1[b][:, 0:1], 0.0)
                else:
                    p.copy("POOL", b0[b][:, 0:2], b0[bp][:, TT:TT + 2])
                    p.copy("POOL", b1[b][:, 0:1], b1[bp][:, TT:TT + 1])
                p.tt("POOL", cv[b], b0[b][:, 0:TT], b1[b][:, 0:TT], ALU.add)
                p.tt("POOL", cv[b], cv[b], b2[b], ALU.add)

            def P3(it, gbase=gbase):
                j, ci, tt = items[it]
                b = (gbase + it) % NB
                tsl = slice(tt * TT, (tt + 1) * TT)
                p.tt("DVE", act[:, j, tsl], sg[b], cv[b], ALU.mult)
                if tt == NTT - 1:
                    wq.pop(ci)

            pipeline(len(items), [P1, P2, P3], [0, 1, 2])
            gbase += len(items)
            G = len(grp)
            dview = down[grp[0] * 128:(grp[0] + G) * 128, :].rearrange("(j p) n -> p j n", p=128)
            wd_next = self.wload(dview[:, :, 0:128])
            for n in range(8):
                wd = wd_next
                if n + 1 < 8:
                    wd_next = self.wload(dview[:, :, (n + 1) * 128:(n + 2) * 128])
                for tt in range(NTT):
                    tsl = slice(tt * TT, (tt + 1) * TT)
                    ps = self.ps[6 + (n * NTT + tt) % 2]
                    for j in range(G):
                        p.mm(ps, wd[:, j, :], act[:, j, tsl], start=(j == 0), stop=(j == G - 1))
                    p.tt("DVE", self.h[:, n, tsl], self.h[:, n, tsl], ps, ALU.add)

    def final_norm_store(self, s):
        p = self.p
        of = p.sb("outf", 8 * S, F32, off=self.hn_off).rearrange("p (c t) -> p c t", c=8)
        self.rmsnorm(P_NFIN, out=of)
        self.store_y(s, of)


def build(nseq, stages):
    k = K(nseq, stages)
    p = k.p
    k.prologue()
    for s in range(nseq):
        k.load_x(s)
        if "mix0" in stages:
            mixer0(k)
        if "ffn0" in stages:
            k.ffn(0)
        if "mix1" in stages:
            mixer1(k)
        if "ffn1" in stages:
            k.ffn(1)
        if "final" in stages:
            k.final_norm_store(s)
        else:
            k.store_y(s, k.h)
    p.finish([f"y{t}" for t in range(NTT)])
    print("instructions", p.n_inst, "waits", p.n_wait, "counts", p.cnt)
    return k


A_Q, A_K, A_V, A_G, A_R, B_Q, B_K, B_V = 0, 256, 512, 1024, 1536, 1552, 2064, 2576
C_A, C_B, D_Q, D_K, D_V = 0, 512, 1024, 1536, 2048


def cm(self, i, rows=slice(0, 128), cols=slice(0, 128)):
    return self.cmask[rows, i * 128:(i + 1) * 128][:, cols]


def proj_fm(self, w, M, tt, ps):
    p = self.p
    tsl = slice(tt * TT, (tt + 1) * TT)
    for k in range(8):
        p.mm(ps[0:M, :], w[:, k, 0:M], self.hn[:, k, tsl], start=(k == 0), stop=(k == 7))


def proj_tm(self, w, N, tb, ps):
    p = self.p
    for k in range(8):
        p.mm(ps[:, 0:N], self.hn[:, k, tb * 128:(tb + 1) * 128], w[:, k, 0:N], start=(k == 0), stop=(k == 7))


def out_proj(self, wname):
    p = self.p
    wv = self.wview(wname)
    nxt = self.wload(wv[:, :, 0:128])
    for n in range(8):
        w = nxt
        if n < 7:
            nxt = self.wload(wv[:, :, (n + 1) * 128:(n + 2) * 128])
        for tt in range(NTT):
            tsl = slice(tt * TT, (tt + 1) * TT)
            ps = self.ps[6 + (n * NTT + tt) % 2]
            for k in range(8):
                p.mm(ps, w[:, k, :], self.mix[:, k, tsl], start=(k == 0), stop=(k == 7))
            p.tt("DVE", self.h[:, n, tsl], self.h[:, n, tsl], ps, ALU.add)


def dsw(self):
    p = self.p
    a0 = self.arena_off
    cosT = p.sb("d_cos", S, F32, off=a0)
    sinT = p.sb("d_sin", S, F32, off=a0 + 8192)
    mstrip = p.sb("d_mstrip", 2176, BF16, off=a0 + 16384)
    o = a0 + 16384 + 4352
    qT = p.sb("d_qT", S, BF16, off=o)
    kTz = [p.sb(f"d_kTz{e}", S, BF16, off=o + 4096 + e * 4096) for e in range(2)]
    o += 4096
    v = p.sb("d_v", S, BF16, off=o + 8192).rearrange("p (b d) -> p b d", b=16)
    t1s = [p.sb(f"d_t1{i}", TT, F32, off=o + 12288 + i * 2048) for i in range(2)]
    t2s = [p.sb(f"d_t2{i}", TT, F32, off=o + 16384 + i * 2048) for i in range(2)]
    p.memset("POOL", kTz[0][64:128, :], 0.0)
    p.memset("POOL", kTz[1][0:64, :], 0.0)
    p.dma("SP", cosT, self.w["cosT"], "cos")
    p.dma("SP", sinT, self.w["sinT"], "sin")
    p.dma("POOL", mstrip, self.w["mstrip"], "ms")
    w0 = self.wview("w_in0")
    wr = self.wview("wrot0")
    NB = 3
    pt = [p.sb(f"d_pt{i}", TT, BF16, off=o + 20480 + i * 1024) for i in range(NB)]
    pm = [p.sb(f"d_pm{i}", TT, BF16, off=o + 23552 + i * 1024) for i in range(NB)]
    rden = t1s[0]
    def dsw_w(pr):
        return (self.wload(w0[:, :, B_Q + pr * 128:B_Q + (pr + 1) * 128]),
                self.wload(wr[:, :, pr * 128:(pr + 1) * 128]),
                self.wload(w0[:, :, B_K + pr * 128:B_K + (pr + 1) * 128]),
                self.wload(wr[:, :, 512 + pr * 128:512 + (pr + 1) * 128]),
                self.wload(w0[:, :, B_V + pr * 128:B_V + (pr + 1) * 128]))
    wnext = dsw_w(0)
    for pr in range(4):
        wq, wqr, wk, wkr, wv = wnext
        pj = [(wa, wb, dst, tt) for (wa, wb, dst) in ((wq, wqr, qT), (wk, wkr, None)) for tt in range(NTT)]

        def J0(it):
            wa, wb, dst, tt = pj[it]
            proj_fm(self, wa, 128, tt, self.ps[0] if it % 2 == 0 else self.ps[5])
            proj_fm(self, wb, 128, tt, self.ps[6 + it % 2])

        def J1(it):
            wa, wb, dst, tt = pj[it]
            tsl = slice(tt * TT, (tt + 1) * TT)
            p.tt("DVE", t1s[it % 2], self.ps[0] if it % 2 == 0 else self.ps[5], cosT[:, tsl], ALU.mult)
            p.tt("DVE", t2s[it % 2], self.ps[6 + it % 2], sinT[:, tsl], ALU.mult)

        def J2(it):
            wa, wb, dst, tt = pj[it]
            tsl = slice(tt * TT, (tt + 1) * TT)
            t1, t2 = t1s[it % 2], t2s[it % 2]
            if dst is not None:
                p.tt("POOL", dst[:, tsl], t1, t2, ALU.add)
            else:
                p.tt("POOL", kTz[0][0:64, tsl], t1[0:64, :], t2[0:64, :], ALU.add)
                p.tt("POOL", kTz[1][64:128, tsl], t1[64:128, :], t2[64:128, :], ALU.add)

        pipeline(len(pj), [J2, J1, J0], [2, 1, 0])
        for tb in range(16):
            ps = self.ps[0] if tb % 2 == 0 else self.ps[5]
            proj_tm(self, wv, 128, tb, ps)
            p.act(v[:, tb, :], ps[:, 0:128], AF.Copy)
        if pr + 1 < 4:
            wnext = dsw_w(pr + 1)
        items = []
        for qt in range(4):
            for e in range(2):
                for kb in range(4 * qt + 4):
                    items.append((qt, kb, e))

        def geom(it):
            qt, kb, e = items[it]
            q0 = max(qt * TT, kb * 128)
            N = (qt + 1) * TT - q0
            return qt, kb, e, slice(64 * e, 64 * e + 64), q0, N, q0 - qt * TT, q0 - kb * 128

        def D0(it):
            qt, kb, e, rows, q0, N, c0, x0 = geom(it)
            pss = self.ps[5 + it % NB]
            p.mm(pss[:, 0:N], kTz[e][:, kb * 128:(kb + 1) * 128], qT[:, q0:q0 + N])

        def D1(it):
            qt, kb, e, rows, q0, N, c0, x0 = geom(it)
            b = it % NB
            p.act(pt[b][:, 0:N], self.ps[5 + b][:, 0:N], AF.Exp, scale=0.125)

        def D2(it):
            qt, kb, e, rows, q0, N, c0, x0 = geom(it)
            b = it % NB
            p.tt("DVE", pm[b][:, 0:N], pt[b][:, 0:N], mstrip[:, x0:x0 + N], ALU.mult)

        def D3(it):
            qt, kb, e, rows, q0, N, c0, x0 = geom(it)
            b = it % NB
            num, den = self.ps[1 + e], self.ps[3 + e]
            nkb = 4 * qt + 4
            p.mm(num[:, c0:c0 + N], v[:, kb, :], pm[b][:, 0:N], start=(kb == 0), stop=(kb == nkb - 1))
            p.mm(den[:, c0:c0 + N], self.ones_b, pm[b][:, 0:N], start=(kb == 0), stop=(kb == nkb - 1))
            if kb == nkb - 1:
                tsl = slice(qt * TT, (qt + 1) * TT)
                p.act(rden[rows, :], den[rows, :], AF.Ln)
                p.act(rden[rows, :], rden[rows, :], AF.Exp, scale=-1.0)
                p.tt("DVE", self.mix[rows, 4 + pr, tsl], num[rows, :], rden[rows, :], ALU.mult)

        pipeline(len(items), [D3, D2, D1, D0], [3, 2, 1, 0])


def gla(self):
    p = self.p
    a0 = self.arena_off
    l_hi = p.sb("g_lhi", 16 * 256, BF16, off=a0).rearrange("p (b d) -> p b d", b=16)
    l_lo = p.sb("g_llo", 16 * 256, BF16, off=a0 + 8192).rearrange("p (b d) -> p b d", b=16)
    o = a0 + 16384
    arT = p.sb("g_arT", S, BF16, off=o)
    pres = [p.sb(f"g_pre{i}", 256, F32, off=o + 4096 + i * 2048) for i in range(2)]
    lfs = [p.sb(f"g_lf{i}", 256, F32, off=o + 5120 + i * 2048) for i in range(2)]
    q_dec = p.sb("g_qdec", S, BF16, off=o)
    k_inv = p.sb("g_kinv", S, BF16, off=o + 4096)
    k_tail = p.sb("g_ktail", 16 * 64, BF16, off=o + 8192).rearrange("p (b d) -> p b d", b=16)
    v = p.sb("g_v", 16 * 128, BF16, off=o + 10240).rearrange("p (b d) -> p b d", b=16)
    e_pos = p.sb("g_epos", TT, F32, off=o + 14336)
    e_neg = p.sb("g_eneg", TT, F32, off=o + 16384)
    sq = p.sb("g_sq", TT, F32, off=o + 14336)
    r = p.sb("g_r", TT, F32, off=o + 16384)
    sgt = p.sb("g_sg", TT, F32, off=o + 18432)
    t1 = p.sb("g_t1", TT, F32, off=o + 20480)
    wgt = [p.sb(f"g_wgt{i}", 64, F32, off=o + 22528 + i * 256) for i in range(2)]
    scm = [p.sb(f"g_scm{i}", 128, BF16, off=o + 23040 + i * 256) for i in range(2)]
    Sf2 = [p.sb(f"g_sf{i}", 128, F32, off=o + 23552 + i * 512) for i in range(2)]
    dec = p.sb("g_dec", 32, F32, off=o + 24576)
    Sall = p.sb("g_sall", 32 * 128, BF16, off=o + 24704).rearrange("p (n d) -> p n d", n=32)
    o_sb = p.sb("g_osb", S, F32, off=self.mix_off + 4 * S * 2)
    w0 = self.wview("w_in0")
    war = self.wload(w0[:, :, A_R:A_R + 16])
    i = self.walloc()
    wa2 = self.wslots[i][0:16, 0:256]
    p.dma("POOL", wa2, self.w["wa2"][0:16, :], f"w{i}")
    for tt in range(NTT):
        ps = self.ps[tt % 2]
        proj_fm(self, war, 16, tt, ps)
        p.act(arT[0:16, tt * TT:(tt + 1) * TT], ps[0:16, :], AF.Copy)
    for tb in range(16):
        ps = self.ps[2 + tb % 2]
        pre, lf = pres[tb % 2], lfs[tb % 2]
        p.mm(ps[:, 0:256], arT[0:16, tb * 128:(tb + 1) * 128], wa2)
        p.tt("DVE", pre, ps[:, 0:256], self.params[:, P_BA:P_BA + 256], ALU.add)
        p.act(pre, pre, AF.Exp, scale=-1.0)
        p.act(lf, pre, AF.Ln, bias=1.0)
        p.act(l_hi[:, tb, :], pre, AF.Ln, bias=1.0)
        p.tt("DVE", l_lo[:, tb, :], lf, l_hi[:, tb, :], ALU.subtract)
    def gla_w(h):
        return (self.wload(w0[:, :, A_Q + h * 64:A_Q + (h + 1) * 64]),
                self.wload(w0[:, :, A_K + h * 64:A_K + (h + 1) * 64]),
                self.wload(w0[:, :, A_V + h * 128:A_V + (h + 1) * 128]),
                self.wload(w0[:, :, A_G + h * 128:A_G + (h + 1) * 128]))
    gnext = gla_w(0)
    for h in range(4):
        hs = slice(h * 64, (h + 1) * 64)
        wq, wk, wv, wg = gnext
        for tt in range(NTT):
            tsl = slice(tt * TT, (tt + 1) * TT)
            pcs = self.ps[3 * (tt % 2)]
            for j in range(4):
                tb = tt * 4 + j
                p.mm(pcs[0:64, j * 128:(j + 1) * 128], l_hi[:, tb, hs], cm(self, 1), start=True, stop=False)
                p.mm(pcs[0:64, j * 128:(j + 1) * 128], l_lo[:, tb, hs], cm(self, 1), start=False, stop=True)
            p.act(e_pos[0:64, :], pcs[0:64, :], AF.Exp, scale=-1.0 / 16)
            p.act(e_neg[0:64, :], pcs[0:64, :], AF.Exp, scale=1.0 / 16)
            pq, pk = self.ps[3 * (tt % 2) + 1], self.ps[3 * (tt % 2) + 2]
            proj_fm(self, wq, 64, tt, pq)
            proj_fm(self, wk, 64, tt, pk)
            p.stt(q_dec[0:64, tsl], pq[0:64, :], 0.125, e_pos[0:64, :], ALU.mult, ALU.mult)
            p.tt("DVE", k_inv[0:64, tsl], pk[0:64, :], e_neg[0:64, :], ALU.mult)
            p.copy("DVE", dec[0:64, tt * 8:(tt + 1) * 8],
                   e_pos[0:64, :].rearrange("p (n c) -> p n c", c=64)[:, :, 63])
        for tb in range(16):
            b = tb % 2
            pkt, pD, pv = (self.ps[3], self.ps[4], self.ps[5]) if tb % 2 == 0 else (self.ps[0], self.ps[1], self.ps[2])
            proj_tm(self, wk, 64, tb, pkt)
            p.mm(pD[:, 0:64], cm(self, 2), l_hi[:, tb, hs], start=True, stop=False)
            p.mm(pD[:, 0:64], cm(self, 2), l_lo[:, tb, hs], start=False, stop=True)
            p.act(wgt[b], pD[:, 0:64], AF.Exp, scale=-1.0 / 16)
            p.tt("DVE", k_tail[:, tb, :], pkt[:, 0:64], wgt[b], ALU.mult)
            proj_tm(self, wv, 128, tb, pv)
            p.act(v[:, tb, :], pv[:, 0:128], AF.Copy)
        if h + 1 < 4:
            gnext = gla_w(h + 1)
        p.memset("DVE", Sf2[0][0:64, :], 0.0)
        for n in range(31):
            tb, j = n // 2, n % 2
            pkv = self.ps[n % 4]
            p.mm(pkv[0:64, 0:128], k_tail[64 * j:64 * j + 64, tb, :], v[64 * j:64 * j + 64, tb, :])
            p.stt(Sf2[(n + 1) % 2][0:64, :], Sf2[n % 2][0:64, :], dec[0:64, n:n + 1], pkv[0:64, 0:128],
                  ALU.mult, ALU.add)
            p.act(Sall[0:64, n + 1, :], Sf2[(n + 1) % 2][0:64, :], AF.Copy)
        for tb in range(16):
            b = tb % 2
            bsl = slice(tb * 128, (tb + 1) * 128)
            psc, pso = self.ps[4 + b], self.ps[6 + b]
            p.mm(psc[:, 0:128], k_inv[0:64, bsl], q_dec[0:64, bsl])
            p.tt("DVE", scm[b], psc[:, 0:128], cm(self, 1), ALU.mult)
            p.mm(pso[:, 0:128], v[:, tb, :], scm[b], start=True, stop=False)
            for j in range(2):
                n = 2 * tb + j
                csl = slice(tb * 128 + 64 * j, tb * 128 + 64 * j + 64)
                if n > 0:
                    p.mm(pso[:, 64 * j:64 * j + 64], Sall[0:64, n, :], q_dec[0:64, csl],
                         start=False, stop=(j == 1))
            p.act(o_sb[:, bsl], pso[:, 0:128], AF.Copy)
        r_all = p.sb(f"g_rall{h}", 4 * TT, F32, off=o).rearrange("p (a t) -> p a t", a=4)
        for tt in range(NTT):
            tsl = slice(tt * TT, (tt + 1) * TT)
            pss = self.ps[2 + tt % 2]
            p.act(sq, o_sb[:, tsl], AF.Square)
            p.mm(pss, self.ones_f, sq)
            p.act(r_all[:, tt, :], pss, AF.Ln, scale=1.0 / 128, bias=EPS)
            p.act(r_all[:, tt, :], r_all[:, tt, :], AF.Exp, scale=-0.5)
        for tt in range(NTT):
            tsl = slice(tt * TT, (tt + 1) * TT)
            pg = self.ps[4 + tt % 2]
            proj_fm(self, wg, 128, tt, pg)
            p.act(sgt, pg, AF.Silu)
            p.stt(t1, o_sb[:, tsl], self.pcol(P_GLAN), r_all[:, tt, :], ALU.mult, ALU.mult)
            p.tt("DVE", self.mix[:, h, tsl], t1, sgt, ALU.mult)


def mixer0(self):
    self.rmsnorm(P_NMIX0)
    gla(self)
    dsw(self)
    out_proj(self, "w_out0")


def conformer(self):
    p = self.p
    a0 = self.arena_off
    cbuf = p.sb("c_cbuf", 4 * 542, BF16, off=a0).rearrange("p (c t) -> p c t", c=4)
    o = a0 + 4352
    ysb = p.sb("c_ysb", 4 * TT, F32, off=o).rearrange("p (c t) -> p c t", c=4)
    sig = p.sb("c_sig", TT, F32, off=o + 8192)
    sq = p.sb("c_sq", TT, F32, off=o + 10240)
    mean = p.sb("c_mean", TT, F32, off=o + 12288)
    msq = p.sb("c_msq", TT, F32, off=o + 14336)
    r = p.sb("c_r", TT, F32, off=o + 16384)
    dg = [p.sb(f"c_dg{i}", 31 * 128, BF16, off=o + 18432 + i * 7936) for i in range(2)]
    w1 = self.wview("w_in1")
    wca = [self.wload(w1[:, :, C_A + c * 128:C_A + (c + 1) * 128]) for c in range(4)]
    wcb = [self.wload(w1[:, :, C_B + c * 128:C_B + (c + 1) * 128]) for c in range(4)]
    def build_dg(dgb, c):
        for k in range(31):
            p.ts("DVE", dgb[:, k * 128:(k + 1) * 128], cm(self, 0), self.pcol(P_CW + c * 31 + k), ALU.mult)

    sigs = [sig, p.sb("c_sig2", TT, F32, off=o + 18432 + 2 * 7936)]
    build_dg(dg[0], 0)

    def C0(it):
        tt, c = it // 4, it % 4
        pa, pb = (self.ps[0], self.ps[1]) if it % 2 == 0 else (self.ps[6], self.ps[7])
        sg_ = sigs[it % 2]
        proj_fm(self, wca[c], 128, tt, pa)
        proj_fm(self, wcb[c], 128, tt, pb)
        p.act(sg_, pb, AF.Sigmoid)
        if tt == 0:
            p.memset("POOL", cbuf[:, c, 0:30], 0.0)
        else:
            p.copy("POOL", cbuf[:, c, 0:30], cbuf[:, c, 512:542])
        p.tt("DVE", cbuf[:, c, 30:542], pa, sg_, ALU.mult)

    def C1(it):
        tt, c = it // 4, it % 4
        tsl = slice(tt * TT, (tt + 1) * TT)
        psum_s, psum_q = self.ps[4], self.ps[5]
        dgb = dg[it % 2]
        if it + 1 < 16:
            build_dg(dg[(it + 1) % 2], (it + 1) % 4)
        py = self.ps[2 + c % 2]
        for k in range(31):
            p.mm(py, dgb[:, k * 128:(k + 1) * 128], cbuf[:, c, k:k + 512], start=(k == 0), stop=(k == 30))
        p.ts("DVE", ysb[:, c, :], py, self.pcol(P_CB + c), ALU.add)
        p.mm(psum_s, self.ones_f, ysb[:, c, :], start=(c == 0), stop=(c == 3))
        p.act(sq, ysb[:, c, :], AF.Square)
        p.mm(psum_q, self.ones_f, sq, start=(c == 0), stop=(c == 3))
        if c == 3:
            p.ts("DVE", mean, psum_s, 1.0 / 512, ALU.mult)
            p.tt("DVE", msq, mean, mean, ALU.mult)
            p.stt(r, psum_q, 1.0 / 512, msq, ALU.mult, ALU.subtract)
            p.act(r, r, AF.Ln, bias=EPS)
            p.act(r, r, AF.Exp, scale=-0.5)
            for cc in range(4):
                p.tt("DVE", ysb[:, cc, :], ysb[:, cc, :], mean, ALU.subtract)
                p.tt("DVE", ysb[:, cc, :], ysb[:, cc, :], r, ALU.mult)
                p.act(self.mix[:, cc, tsl], ysb[:, cc, :], AF.Silu, scale=self.pcol(P_LG + cc),
                      bias=self.pcol(P_LB + cc))

    pipeline(16, [C0, C1], [0, 1])


def pipeline(n, stages, skews):
    mx = max(skews)
    for step in range(n + mx):
        for f, sk in zip(stages, skews):
            it = step - sk
            if 0 <= it < n:
                f(it)


def stickbreak(self):
    p = self.p
    a0 = self.arena_off
    qT = p.sb("s_qT", S, BF16, off=a0)
    kTz = [p.sb(f"s_kTz{e}", S, BF16, off=a0 + 4096 + e * 4096) for e in range(2)]
    v = p.sb("s_v", S, BF16, off=a0 + 12288).rearrange("p (b d) -> p b d", b=16)
    o = a0 + 16384
    p.memset("POOL", kTz[0][64:128, :], 0.0)
    p.memset("POOL", kTz[1][0:64, :], 0.0)
    NB = 3
    E = [p.sb(f"s_E{i}", TT, F32, off=o + i * 2048) for i in range(NB)]
    SPb = [p.sb(f"s_SP{i}", TT, BF16, off=o + 6144 + i * 1024) for i in range(NB)]
    R = [p.sb(f"s_R{i}", TT, F32, off=o + 9216 + i * 2048) for i in range(2)]
    arg = [p.sb(f"s_arg{i}", TT, F32, off=o + 13312 + i * 2048) for i in range(NB)]
    at = [p.sb(f"s_a{i}", TT, BF16, off=o + 19456 + i * 1024) for i in range(NB)]
    zeros_b = p.sb("s_zeros", 128, BF16, off=o + 22528)
    p.memset("DVE", zeros_b, 0.0)
    w1 = self.wview("w_in1")
    def sb_w(pr):
        return (self.wload(w1[:, :, D_Q + pr * 128:D_Q + (pr + 1) * 128]),
                self.wload(w1[:, :, D_K + pr * 128:D_K + (pr + 1) * 128]),
                self.wload(w1[:, :, D_V + pr * 128:D_V + (pr + 1) * 128]))
    snext = sb_w(0)
    for pr in range(4):
        wq, wk, wv = snext
        for (wa, dst) in ((wq, qT), (wk, None)):
            for tt in range(NTT):
                ps = self.ps[tt % 2]
                tsl = slice(tt * TT, (tt + 1) * TT)
                proj_fm(self, wa, 128, tt, ps)
                if dst is not None:
                    p.act(dst[:, tsl], ps, AF.Copy)
                else:
                    p.act(kTz[0][0:64, tsl], ps[0:64, :], AF.Copy)
                    p.act(kTz[1][64:128, tsl], ps[64:128, :], AF.Copy)
        for tb in range(16):
            ps = self.ps[tb % 2]
            proj_tm(self, wv, 128, tb, ps)
            p.act(v[:, tb, :], ps[:, 0:128], AF.Copy)
        if pr + 1 < 4:
            snext = sb_w(pr + 1)
        items = []
        for qt in range(4):
            for e in range(2):
                for kb in range(4 * qt + 3, -1, -1):
                    items.append((qt, kb, e))

        def geom(it):
            qt, kb, e = items[it]
            q0 = max(qt * TT, kb * 128)
            N = (qt + 1) * TT - q0
            return qt, kb, e, slice(64 * e, 64 * e + 64), q0, N, q0 - qt * TT, kb >= 4 * qt

        def S0(it):
            qt, kb, e, rows, q0, N, c0, diag = geom(it)
            pz = self.ps[it % 4]
            p.mm(pz[:, 0:N], kTz[e][:, kb * 128:(kb + 1) * 128], qT[:, q0:q0 + N], start=True, stop=not diag)
            if diag:
                p.mm(pz[:, 0:128], cm(self, 0), cm(self, 7), start=False, stop=True)

        def S1(it):
            qt, kb, e, rows, q0, N, c0, diag = geom(it)
            b = it % NB
            pz = self.ps[it % 4]
            p.act(E[b][:, 0:N], pz[:, 0:N], AF.Exp, scale=0.125)
            p.act(SPb[b][:, 0:N], E[b][:, 0:N], AF.Ln, bias=1.0)

        def S2(it):
            qt, kb, e, rows, q0, N, c0, diag = geom(it)
            b = it % NB
            pz, prr = self.ps[it % 4], self.ps[4 + it % 2]
            p.mm(pz[:, 0:N], cm(self, 4), SPb[b][:, 0:N], start=False, stop=True, skip=True)
            p.mm(prr[:, 0:N], cm(self, 6), SPb[b][:, 0:N])

        def S3(it):
            qt, kb, e, rows, q0, N, c0, diag = geom(it)
            b = it % NB
            pz, prr = self.ps[it % 4], self.ps[4 + it % 2]
            if kb == 4 * qt + 3:
                p.memset("POOL", R[e], 0.0)
            p.stt(arg[b][:, 0:N], pz[:, 0:N], 0.125, R[e][:, c0:c0 + N], ALU.mult, ALU.subtract)
            p.tt("DVE", R[e][:, c0:c0 + N], R[e][:, c0:c0 + N], prr[:, 0:N], ALU.add)

        def S4(it):
            qt, kb, e, rows, q0, N, c0, diag = geom(it)
            p.act(at[it % NB][:, 0:N], arg[it % NB][:, 0:N], AF.Exp)

        def S5(it):
            qt, kb, e, rows, q0, N, c0, diag = geom(it)
            po = self.ps[6 + e]
            if kb == 4 * qt + 3:
                p.mm(po, zeros_b, self.cmask[:, 0:512], start=True, stop=False)
            p.mm(po[:, c0:c0 + N], v[:, kb, :], at[it % NB][:, 0:N], start=False, stop=(kb == 0))
            if kb == 0:
                p.act(self.mix[rows, 4 + pr, qt * TT:(qt + 1) * TT], po[rows, :], AF.Copy)

        pipeline(len(items), [S5, S4, S3, S2, S1, S0], [5, 4, 3, 2, 1, 0])


def mixer1(self):
    self.rmsnorm(P_NMIX1)
    if "noconf" not in self.stages:
        conformer(self)
    if "nosb" not in self.stages:
        stickbreak(self)
    out_proj(self, "w_out1")


import numpy as np

def pcols(v):
    v = np.asarray(v, np.float32)
    return np.ascontiguousarray(v.reshape(-1, 128).T)

def host_params(I):
    P = np.zeros((128, NPAR), np.float32)
    P[:, P_NMIX0:P_NMIX0 + 8] = pcols(I["norm_mix0"])
    P[:, P_NFFN0:P_NFFN0 + 8] = pcols(I["norm_ffn0"])
    P[:, P_NMIX1:P_NMIX1 + 8] = pcols(I["norm_mix1"])
    P[:, P_NFFN1:P_NFFN1 + 8] = pcols(I["norm_ffn1"])
    P[:, P_NFIN:P_NFIN + 8] = pcols(I["final_norm"])
    P[:, P_GLAN] = np.asarray(I["gla_norm"], np.float32)
    for L, off in ((0, P_FC0), (1, P_FC1)):
        fc = np.asarray(I[f"ffn_conv{L}"], np.float32)
        a = fc.reshape(3, 44, 128).transpose(2, 1, 0)
        P[:, off:off + 132] = a.reshape(128, 132)
    cw = np.asarray(I["conv_w1"], np.float32)
    P[:, P_CW:P_CW + 124] = cw.reshape(31, 4, 128).transpose(2, 1, 0).reshape(128, 124)
    P[:, P_CB:P_CB + 4] = pcols(I["conv_b1"])
    P[:, P_LG:P_LG + 4] = pcols(I["conv_ln_g1"])
    P[:, P_LB:P_LB + 4] = pcols(I["conv_ln_b1"])
    P[:, P_BA:P_BA + 256] = np.asarray(I["gla_ba"], np.float32)[None, :]
    return P

def host_consts():
    C = {}
    half = 8
    inv = (np.float32(500000.0) ** (-np.arange(half, dtype=np.float32) / np.float32(half))).astype(np.float32)
    ang = (np.arange(S, dtype=np.float32)[:, None] * inv[None, :]).astype(np.float32)
    cos, sin = np.cos(ang).astype(np.float32), np.sin(ang).astype(np.float32)
    cosT = np.ones((128, S), np.float32)
    sinT = np.zeros((128, S), np.float32)
    for base in (0, 64):
        cosT[base:base + 8] = cos.T
        cosT[base + 8:base + 16] = cos.T
        sinT[base:base + 8] = -sin.T
        sinT[base + 8:base + 16] = sin.T
    C["cosT"], C["sinT"] = cosT, sinT
    ki = np.arange(128)[:, None]
    x = np.arange(2176)[None, :]
    d = x - ki
    c = ((d >= 0) & (d <= 128)).astype(np.float32) + ((d >= 0) & (d % 4 == 0) & (d <= 512)) + ((d >= 0) & (d % 16 == 0) & (d <= 2048))
    C["mstrip"] = c.astype(np.float32)
    cm = np.zeros((128, 8, 128), np.float32)
    i = np.arange(128)[:, None]
    j = np.arange(128)[None, :]
    cm[:, 0] = (i == j)
    cm[:, 1] = ((i // 64) == (j // 64)) & (i <= j)
    cm[:, 2] = ((i // 64) == (j // 64)) & (i > j)
    cm[:, 3] = ((i // 64) == (j // 64)) & (i <= j)
    cm[:, 4] = np.where(i >= j, -8.0, 0.0)
    cm[:, 5] = (i >= j)
    cm[:, 6] = 1.0
    cm[:, 7] = np.where(i >= j, -240000.0, 0.0)
    C["cmask"] = cm.reshape(128, 1024)
    return C

def host_inmap(I, seqs):
    x = np.asarray(I["x"], np.float32)
    m = {"xT": np.ascontiguousarray(x[seqs].transpose(0, 2, 1)), "params": host_params(I)}
    for n in ["w_in0", "w_out0", "ffn_up0", "ffn_down0", "w_in1", "w_out1", "ffn_up1", "ffn_down1"]:
        m[n] = np.ascontiguousarray(np.asarray(I[n], np.float32))
    w = m["w_in0"]
    GO = 1552
    def rot(base):
        cols = []
        for h in range(8):
            b = base + h * 64
            idx = np.arange(b, b + 64)
            idx[0:8] = np.arange(b + 8, b + 16)
            idx[8:16] = np.arange(b, b + 8)
            cols.append(idx)
        return np.concatenate(cols)
    m["wrot0"] = np.ascontiguousarray(np.concatenate([w[:, rot(GO)], w[:, rot(GO + 512)]], axis=1))
    wa2 = np.zeros((32, 256), np.float32)
    wa2[:16] = np.asarray(I["gla_wa2"], np.float32)
    m["wa2"] = wa2
    m.update(host_consts())
    return m


def kernel(**inputs):
    I = {k: np.asarray(v) for k, v in inputs.items()}
    B = I["x"].shape[0]
    ncores = 8
    per = B // ncores
    k = build(per, ["mix0", "ffn0", "mix1", "ffn1", "final"])
    base = host_inmap(I, list(range(per)))
    x = np.asarray(I["x"], np.float32)
    in_maps = []
    for c in range(ncores):
        m = dict(base)
        m["xT"] = np.ascontiguousarray(x[c * per:(c + 1) * per].transpose(0, 2, 1))
        in_maps.append(m)
    res = run_bass_kernel_spmd(k.p.nc, in_maps, core_ids=list(range(ncores)))
    out = np.empty((B, S, D), np.float32)
    for c in range(ncores):
        yT = np.asarray(res.results[c]["yT"])
        out[c * per:(c + 1) * per] = yT.transpose(0, 2, 1)
    return out
```
